# Optimizing a Trainium2 kernel written in Bass

```python
import math
import jax, jax.numpy as jnp
from jax import lax
import numpy as np

D_MODEL = 1024
BATCH = 2
SEQ = 8192
DEPTH = 2

D_MIX = 3 * D_MODEL // 2
W_A = D_MIX // 3
W_B = D_MIX // 3
W_C = D_MIX - W_A - W_B
HEAD_DIM = 64
N_HEADS_A = W_A // HEAD_DIM
N_HEADS_B = W_B // HEAD_DIM
POOL_WINDOWS = (2, 4, 8, 16)
N_POOL_GROUPS = len(POOL_WINDOWS)
POOL_GROUP_DIM = W_C // N_POOL_GROUPS
CONV_A_WIDTH = 3
CONV_B_WIDTH = 31
D_IN = 4 * W_A + 3 * W_B + 2 * W_C
DEEPNORM_ALPHA = (2.0 * DEPTH) ** 0.25
DEEPNORM_BETA = (8.0 * DEPTH) ** -0.25
LN_EPS = 1e-5

kernel_name = "hybrid_conv_pool_deepnorm_trunk"


def layer_norm(x, g, b):
    xf = x.astype(jnp.float32)
    mu = jnp.mean(xf, axis=-1, keepdims=True)
    xc = xf - mu
    var = jnp.mean(xc * xc, axis=-1, keepdims=True)
    y = xc * lax.rsqrt(var + LN_EPS) * g.astype(jnp.float32) + b.astype(jnp.float32)
    return y.astype(x.dtype)


def causal_depthwise_conv(u, w, b):
    k, c = w.shape
    y = lax.conv_general_dilated(
        u, w[:, None, :].astype(u.dtype),
        window_strides=(1,), padding=[(k - 1, 0)],
        dimension_numbers=("NWC", "WIO", "NWC"),
        feature_group_count=c)
    return y + b.astype(u.dtype)


def multiscale_causal_pool(u):
    t = u.shape[1]
    uf = u.astype(jnp.float32)
    cs = jnp.cumsum(uf, axis=1)
    pos = jnp.arange(t, dtype=jnp.float32)[None, :, None]
    outs = []
    for g, w in enumerate(POOL_WINDOWS):
        sl = slice(g * POOL_GROUP_DIM, (g + 1) * POOL_GROUP_DIM)
        cs_g = cs[..., sl]
        cs_shift = jnp.pad(cs_g[:, : t - w], ((0, 0), (w, 0), (0, 0)))
        count = jnp.minimum(pos + 1.0, float(w))
        mean = (cs_g - cs_shift) / count
        outs.append(mean - uf[..., sl])
    return jnp.stack(outs, axis=2).astype(u.dtype)


def hybrid_layer(x, w_in, conv_a_w, conv_a_b, conv_b_w, conv_b_b, ln_b_g, ln_b_b,
                 pool_w, pool_b, pool_scale, w_out, ln_g, ln_b):
    bsz, t, _ = x.shape
    h = jnp.einsum("btd,de->bte", x, w_in)
    splits = np.cumsum([W_A, W_A, W_A, W_A, W_B, W_B, W_B, W_C])
    a_bg, a_cg, a_v, a_z, b_v, b_g, b_z, c_u, c_z = jnp.split(h, splits, axis=-1)

    y_a = a_bg * causal_depthwise_conv(a_cg * a_v, conv_a_w, conv_a_b)
    y_a = y_a * jax.nn.silu(a_z)

    u_b = b_v * jax.nn.sigmoid(b_g)
    u_b = causal_depthwise_conv(u_b, conv_b_w, conv_b_b)
    u_b = jax.nn.silu(layer_norm(u_b, ln_b_g, ln_b_b))
    y_b = u_b * jax.nn.silu(b_z)

    p = multiscale_causal_pool(c_u)
    p = jnp.einsum("btgc,gcd->btgd", p, pool_w) + pool_b
    y_c = p.reshape(bsz, t, W_C) * pool_scale
    y_c = y_c * jax.nn.silu(c_z)

    y = jnp.concatenate([y_a, y_b, y_c], axis=-1)
    out = jnp.einsum("bte,ed->btd", y, w_out)
    return layer_norm(DEEPNORM_ALPHA * x + out, ln_g, ln_b)


def setup_inputs(seed: int = 0) -> dict:
    key = jax.random.key(seed)
    ks = jax.random.split(key, 16)
    f32 = jnp.float32
    nrm = lambda k, s, sc: jax.random.normal(k, s, f32) * sc
    return {
        "x": jax.random.normal(ks[0], (BATCH, SEQ, D_MODEL), f32),
        "w_in": nrm(ks[1], (DEPTH, D_MODEL, D_IN), D_MODEL ** -0.5),
        "conv_a_w": nrm(ks[2], (DEPTH, CONV_A_WIDTH, W_A), CONV_A_WIDTH ** -0.5),
        "conv_a_b": nrm(ks[3], (DEPTH, W_A), 0.02),
        "conv_b_w": nrm(ks[4], (DEPTH, CONV_B_WIDTH, W_B), CONV_B_WIDTH ** -0.5),
        "conv_b_b": nrm(ks[5], (DEPTH, W_B), 0.02),
        "ln_b_g": 1.0 + nrm(ks[6], (DEPTH, W_B), 0.02),
        "ln_b_b": nrm(ks[7], (DEPTH, W_B), 0.02),
        "pool_w": nrm(ks[8], (DEPTH, N_POOL_GROUPS, POOL_GROUP_DIM, POOL_GROUP_DIM), POOL_GROUP_DIM ** -0.5),
        "pool_b": nrm(ks[9], (DEPTH, N_POOL_GROUPS, POOL_GROUP_DIM), 0.02),
        "pool_scale": 1.0 + nrm(ks[10], (DEPTH, W_C), 0.02),
        "w_out": nrm(ks[11], (DEPTH, D_MIX, D_MODEL), DEEPNORM_BETA * D_MIX ** -0.5),
        "ln_g": 1.0 + nrm(ks[12], (DEPTH, D_MODEL), 0.02),
        "ln_b": nrm(ks[13], (DEPTH, D_MODEL), 0.02),
    }


def reference(x, w_in, conv_a_w, conv_a_b, conv_b_w, conv_b_b, ln_b_g, ln_b_b,
              pool_w, pool_b, pool_scale, w_out, ln_g, ln_b):
    for l in range(DEPTH):
        x = hybrid_layer(x, w_in[l], conv_a_w[l], conv_a_b[l], conv_b_w[l], conv_b_b[l],
                         ln_b_g[l], ln_b_b[l], pool_w[l], pool_b[l], pool_scale[l],
                         w_out[l], ln_g[l], ln_b[l])
    return x
```

```python
import contextlib
import numpy as np
import concourse.bass as bass
import concourse.mybir as mybir
from concourse.bass_utils import run_bass_kernel_spmd

F32 = mybir.dt.float32
BF16 = mybir.dt.bfloat16
AF = mybir.ActivationFunctionType
ALU = mybir.AluOpType

D_MODEL = 1024
BATCH = 2
SEQ = 8192
DEPTH = 2
D_IN = 4608
D_MIX = 1536
NCORES = 8
TPC = BATCH * SEQ // NCORES
NBLK = 2
TB = TPC // NBLK
HALO = 64
TBH = TB + HALO
ALPHA = (2.0 * DEPTH) ** 0.25
LN_EPS = 1e-5
POOL_W = (2, 4, 8, 16)
NPAR = 9

TILES = [(0, 64), (64, 512), (576, 512)]
CHUNKS = [(0, 64)] + [(64 + 128 * i, 128) for i in range(8)]
TILE_CHUNKS = [[0], [1, 2, 3, 4], [5, 6, 7, 8]]
CHUNK_TILE = [0, 1, 1, 1, 1, 2, 2, 2, 2]

COLS = []
for g in range(4):
    COLS += [2048 + 128 * g, 2560 + 128 * g, 3072 + 128 * g]
for g in range(4):
    COLS += [0 + 128 * g, 512 + 128 * g, 1024 + 128 * g, 1536 + 128 * g]
CG = (3, 2, 1, 0)
for g in CG:
    COLS += [3584 + 128 * g, 4096 + 128 * g]

ENGS = ("pe", "act", "dve", "pool", "sp")
NDMASEM = 24


class Res:
    __slots__ = ("name", "w", "r")

    def __init__(self, name):
        self.name = name
        self.w = None
        self.r = {}


class Op:
    __slots__ = ("eng", "fn", "deps", "idx", "dma", "dma_id")

    def __init__(self, eng, fn, dma):
        self.eng = eng
        self.fn = fn
        self.deps = set()
        self.dma = dma
        self.dma_id = None


class Prog:
    def __init__(self):
        self.ops = {e: [] for e in ENGS}
        self.ndma_k = [0, 0]
        self.signal = {e: set() for e in ENGS}

    def add(self, eng, fn, reads=(), writes=(), dma=False):
        op = Op(eng, fn, dma)
        op.idx = len(self.ops[eng])
        if dma:
            kind = 1 if eng == "pool" else 0
            op.dma_id = 2 * self.ndma_k[kind] + kind
            self.ndma_k[kind] += 1
            me = ("dma", op.dma_id)
            if op.dma_id >= 2 * NDMASEM:
                op.deps.add(("dma", op.dma_id - 2 * NDMASEM))
        else:
            me = (eng, op.idx)
        for r in reads:
            if r.w is not None:
                op.deps.add(r.w)
        for w in writes:
            if w.w is not None:
                op.deps.add(w.w)
            for e, i in w.r.items():
                if e == "dma":
                    for j in i:
                        op.deps.add(("dma", j))
                else:
                    op.deps.add((e, i))
        for r in reads:
            if dma:
                r.r.setdefault("dma", set()).add(op.dma_id)
            else:
                r.r[eng] = op.idx
        for w in writes:
            w.w = me
            w.r = {}
        if eng == "pe" and not dma:
            op.deps = {d for d in op.deps if d[0] != "pe"}
        op.deps.discard(me)
        for d in op.deps:
            if d[0] != "dma":
                self.signal[d[0]].add(d[1])
        self.ops[eng].append(op)
        return op

    def emit(self, nc, final_wait_dmas=()):
        with contextlib.ExitStack() as st:
            sems = {e: st.enter_context(nc.semaphore("s_" + e)) for e in ENGS}
            dsem = [st.enter_context(nc.semaphore("d%d" % i)) for i in range(2 * NDMASEM)]
            block = st.enter_context(nc.Block())
            sigcnt = {}
            for e in ENGS:
                c = 0
                m = {}
                for op in self.ops[e]:
                    if op.idx in self.signal[e]:
                        c += 1
                        m[op.idx] = c
                sigcnt[e] = m
            starters = {"pe": block.tensor, "act": block.scalar, "dve": block.vector,
                        "pool": block.gpsimd, "sp": block.sync}
            for e in ENGS:
                ops = self.ops[e]
                if not ops:
                    continue

                def body(engh, e=e, ops=ops):
                    waited = {}
                    for op in ops:
                        need = {}
                        for d in op.deps:
                            if d[0] == "dma":
                                key = ("d", d[1] % (2 * NDMASEM))
                                val = 16 * (d[1] // (2 * NDMASEM) + 1)
                            else:
                                key = ("e", d[0])
                                val = sigcnt[d[0]][d[1]]
                            if need.get(key, 0) < val:
                                need[key] = val
                        for key, val in need.items():
                            if waited.get(key, 0) >= val:
                                continue
                            waited[key] = val
                            engh.wait_ge(dsem[key[1]] if key[0] == "d" else sems[key[1]], val)
                        ins = op.fn(engh)
                        if op.dma:
                            ins.then_inc(dsem[op.dma_id % (2 * NDMASEM)], 16)
                        elif op.idx in self.signal[e]:
                            ins.then_inc(sems[e], 1)
                    if e == "sp":
                        for j in final_wait_dmas:
                            engh.wait_ge(dsem[j % (2 * NDMASEM)], 16 * (j // (2 * NDMASEM) + 1))

                starters[e](body)


class Buf:
    __slots__ = ("ap", "res")

    def __init__(self, ap, res):
        self.ap = ap
        self.res = res


def build_nc(layers, debug=False):
    NL = len(layers)
    nc = bass.Bass("TRN2", target_bir_lowering=False)
    xin = nc.dram_tensor("xin", [NBLK, TBH, D_MODEL], F32, kind="ExternalInput").ap()
    w_in_d = nc.dram_tensor("w_in_r", [NL, 36, 128, 1024], F32, kind="ExternalInput").ap()
    w_out_d = nc.dram_tensor("w_out_r", [NL, 12, 128, 1024], F32, kind="ExternalInput").ap()
    pool_w_d = nc.dram_tensor("pool_w_r", [NL, 128, 512], F32, kind="ExternalInput").ap()
    par_d = nc.dram_tensor("params", [NL, 128, 4 * NPAR], F32, kind="ExternalInput").ap()
    lnbc_d = nc.dram_tensor("lnbc", [NL, 128, 2048], F32, kind="ExternalInput").ap()
    ident_d = nc.dram_tensor("ident", [128, 128], F32, kind="ExternalInput").ap()
    w4_d = nc.dram_tensor("w4", [NL, 128, 128], F32, kind="ExternalInput").ap()
    dmask_d = nc.dram_tensor("dmask", [128, 32], F32, kind="ExternalInput").ap()
    aux_d = nc.dram_tensor("aux", [128, 2 + 32], F32, kind="ExternalInput").ap()
    out_d = nc.dram_tensor("out", [NBLK, TB, D_MODEL], F32, kind="ExternalOutput").ap()
    if debug:
        dbg_y = nc.dram_tensor("dbg_y", [NBLK * NL, 128, 12 * TBH], BF16, kind="ExternalOutput").ap()
        dbg_x = nc.dram_tensor("dbg_x", [NBLK * NL, 128, 8 * TBH], BF16, kind="ExternalOutput").ap()

    P = Prog()
    NSLOT = 56
    with contextlib.ExitStack() as st:
        def sb(name, shape, dt):
            return st.enter_context(nc.sbuf_tensor(name, shape, dt))

        Rb_t = sb("Rb", [128, 9, 1024], F32)
        xT_t = sb("xT", [128, 8, TBH], BF16)
        yb_t = sb("yb", [128, 12, TBH], BF16)
        wrg_t = sb("wrg", [128, 9, 8, 128], BF16)
        wo_t = sb("wo", [128, 12, 1024], BF16)
        Lm_t = sb("Lm", [128, 4, 8, 32], BF16)
        U_t = sb("Urep", [128, 4, 32 + TBH], BF16)
        w4_t = sb("w4s", [128, 2, 128], F32)
        dm_t = sb("dmask_s", [128, 32], F32)
        pw_t = sb("pw", [128, 2, 4, 128], BF16)
        par_t = sb("par", [128, 2, 4, NPAR], F32)
        lnbc_t = sb("lnbc_s", [128, 2, 1024], F32)
        ids_t = sb("ids", [128, 128], F32)
        idb_t = sb("idb", [128, 128], BF16)
        ones_t = sb("ones", [128, 128], BF16)
        aux_t = sb("aux_s", [128, 34], F32)
        corr_t = sb("corr", [128, 2, 4, 16], F32)
        cnt_t = sb("cnt", [128, 16], F32)
        eps_t = sb("epst", [128, 1], F32)
        tA_t = sb("tailA", [128, NL, 4, 2], F32)
        tB_t = sb("tailB", [128, NL, 4, 30], BF16)
        tC_t = sb("tailC", [128, NL, 4, 15], F32)
        st_t = sb("stt", [128, 2, 2, 6], F32)
        mv_t = sb("mv", [128, 2, 4], F32)
        sd_t = sb("sdt", [128, 2, 2], F32)
        ar_t = sb("arena", [128, NSLOT * 256], F32)
        p2b_t = sb("p2xb", [128, 3, 1024], BF16)
        ps = [st.enter_context(nc.psum_tensor("ps%d" % i, [128, 512], F32)) for i in range(7)]
        pT_t = st.enter_context(nc.psum_tensor("pT", [128, 8, 128], BF16))

        R_Rb = [Res("Rb%d" % i) for i in range(9)]
        R_xT = [Res("xT%d" % i) for i in range(9)]
        R_y = [[Res("y%d_%d" % (c, t)) for t in range(3)] for c in range(12)]
        R_wrg = [Res("wrg%d" % i) for i in range(9)]
        R_wo = [Res("wo%d" % i) for i in range(12)]
        R_dg = Res("Lm")
        R_U = [Res("U0"), Res("U1"), Res("U2")]
        R_w4 = [Res("w4_0"), Res("w4_1")]
        R_dm = Res("dmask")
        R_pw = [Res("pw0"), Res("pw1")]
        R_par = [Res("par0"), Res("par1")]
        R_lnbc = Res("lnbc")
        R_ids, R_idb, R_ones, R_aux, R_corr, R_cnt = (Res("ids"), Res("idb"), Res("ones"), Res("aux"),
                                                      Res("corr"), Res("cnt"))
        R_eps = Res("eps")
        R_tA = [[Res("tA%d_%d" % (l, g)) for g in range(4)] for l in range(NL)]
        R_tB = [[Res("tB%d_%d" % (l, g)) for g in range(4)] for l in range(NL)]
        R_tC = [[Res("tC%d_%d" % (l, g)) for g in range(4)] for l in range(NL)]
        R_st = [Res("st0"), Res("st1")]
        R_mv = [Res("mv0"), Res("mv1")]
        R_sd = [Res("sd0"), Res("sd1")]
        R_slot = [Res("slot%d" % i) for i in range(NSLOT)]
        R_p2 = [Res("p2xn0"), Res("p2xn1")]
        R_p2b = [Res("p2xb0"), Res("p2xb1"), Res("p2xb2")]
        R_ps = [Res("ps%d" % i) for i in range(7)]
        R_pT = Res("pT")

        def arena(s0, ns, dt, n):
            ap = ar_t[:, s0 * 256:(s0 + ns) * 256]
            if dt is BF16:
                ap = ap.bitcast(BF16)
            assert n <= ns * 256 * (2 if dt is BF16 else 1)
            return Buf(ap[:, 0:n], R_slot[s0:s0 + ns])

        bank_ctr = [0]

        def new_bank():
            i = bank_ctr[0] % 7
            bank_ctr[0] += 1
            return Buf(ps[i], [R_ps[i]])

        jobs = []

        def job_state():
            return {"dma": 0, "cast": 0}
        js = job_state()

        NRING = 9

        def pump(upto_job):
            while js["dma"] < len(jobs) and js["dma"] <= upto_job:
                jb = jobs[js["dma"]]
                P.add("pool", lambda e, jb=jb: e.dma_start(out=jb["dst"], in_=jb["src"]), writes=jb["res"], dma=True)
                js["dma"] += 1

        horizon = [-1]

        def pump_tick(n):
            while n > 0 and js["dma"] < len(jobs) and js["dma"] <= horizon[0]:
                jb = jobs[js["dma"]]
                P.add("pool", lambda e, jb=jb: e.dma_start(out=jb["dst"], in_=jb["src"]), writes=jb["res"], dma=True)
                js["dma"] += 1
                n -= 1

        def pump_group(lbi, ci0):
            s0 = lbi * 36 + ci0
            horizon[0] = max(horizon[0], stream_job[min(s0 + NRING - 1, len(stream_job) - 1)])
            need = stream_job[min(s0 + 3, len(stream_job) - 1)]
            while js["dma"] <= need and js["dma"] <= horizon[0]:
                pump_tick(1)
            pump_tick(1)

        win_job = {}
        stream_job = []
        nstream = 0
        LB = [(blk, li) for blk in range(NBLK) for li in range(NL)]
        for lbi, (blk, li) in enumerate(LB):
            jobs.append(dict(src=pool_w_d[li], n=512,
                             dst=pw_t[:, lbi % 2].rearrange("p g d -> p (g d)"), res=[R_pw[lbi % 2]]))
            for ci in range(36):
                slot = nstream % NRING
                nstream += 1
                win_job[(lbi, ci)] = (len(jobs), slot)
                stream_job.append(len(jobs))
                jobs.append(dict(src=w_in_d[li, ci], n=1024,
                                 dst=wrg_t[:, slot].rearrange("p k c -> p (k c)"), res=[R_wrg[slot]]))
                if 12 <= ci < 24:
                    ec = ci - 12
                    jobs.append(dict(src=w_out_d[li, ec], n=1024, dst=wo_t[:, ec, :], res=[R_wo[ec]]))

        P.add("sp", lambda e: e.dma_start(out=ids_t[:], in_=ident_d), writes=[R_ids], dma=True)
        P.add("sp", lambda e: e.dma_start(out=aux_t[:], in_=aux_d), writes=[R_aux], dma=True)
        P.add("sp", lambda e: e.dma_start(out=dm_t[:], in_=dmask_d), writes=[R_dm], dma=True)
        P.add("pool", lambda e: e.memset(U_t[:, :, 0:32], 0.0), writes=[R_U[0]])
        P.add("pool", lambda e: e.tensor_copy(out=idb_t[:], in_=ids_t[:]), reads=[R_ids], writes=[R_idb])
        P.add("pool", lambda e: e.memset(ones_t[:], 1.0), writes=[R_ones])
        P.add("pool", lambda e: e.memset(eps_t[:], LN_EPS), writes=[R_eps])
        for j in range(NBLK):
            for g in range(4):
                w = float(POOL_W[g])
                P.add("dve", lambda e, j=j, w=w: e.tensor_scalar(out=cnt_t[:], in0=aux_t[:, 2 + 16 * j:2 + 16 * (j + 1)],
                                                                  scalar1=1.0, scalar2=w, op0=ALU.add, op1=ALU.min),
                      reads=[R_aux], writes=[R_cnt])
                P.add("dve", lambda e: e.reciprocal(out=cnt_t[:], in_=cnt_t[:]), reads=[R_cnt], writes=[R_cnt])
                P.add("dve", lambda e, j=j, g=g, w=w: e.tensor_scalar(out=corr_t[:, j, g, :], in0=cnt_t[:], scalar1=w,
                                                                      scalar2=None, op0=ALU.mult),
                      reads=[R_cnt], writes=[R_corr])

        out_dmas = []
        p2ctr = [0]

        def transposes_to_xT(src_bf, src_res, tc, blk):
            q0, n = CHUNKS[tc]
            for k in range(8):
                P.add("pe", lambda e, k=k, n=n: e.transpose(out=pT_t[:, k, 0:n], in_=src_bf[0:n, k * 128:(k + 1) * 128],
                                                            identity=idb_t[0:n, 0:n]),
                      reads=src_res + [R_idb], writes=[R_pT] if k in (0, 7) else [])
            if tc == 0:
                P.add("act", lambda e, n=n, q0=q0: e.activation(out=xT_t[:, :, q0:q0 + n], in_=pT_t[:, :, 0:n],
                                                                 func=AF.Identity, scale=aux_t[:, blk:blk + 1]),
                      reads=[R_pT, R_aux], writes=[R_xT[tc]])
            else:
                P.add("act", lambda e, n=n, q0=q0: e.activation(out=xT_t[:, :, q0:q0 + n], in_=pT_t[:, :, 0:n],
                                                                 func=AF.Copy),
                      reads=[R_pT], writes=[R_xT[tc]])

        p0_pp = {}
        p0ctr = [0]

        def p0_load_f32(blk, tc):
            q0, n = CHUNKS[tc]
            P.add("sp", lambda e, tc=tc, q0=q0, n=n, blk=blk: e.dma_start(out=Rb_t[0:n, tc, :], in_=xin[blk, q0:q0 + n, :]),
                  writes=[R_Rb[tc]], dma=True)

        def p0_load_bf16(blk, tc):
            q0, n = CHUNKS[tc]
            pp = p0ctr[0] % 3
            p0ctr[0] += 1
            p0_pp[(blk, tc)] = pp
            P.add("pool", lambda e, q0=q0, n=n, pp=pp, blk=blk: e.dma_start(out=p2b_t[0:n, pp, :], in_=xin[blk, q0:q0 + n, :]),
                  writes=[R_p2b[pp]], dma=True)

        def p0_transpose(blk, tc):
            pp = p0_pp[(blk, tc)]
            transposes_to_xT(p2b_t[:, pp, :], [R_p2b[pp]], tc, blk)

        early_done = set()
        diag0_done = set()
        carry_tr = []

        def emit_par_load(l2):
            li2 = LB[l2][1]
            par2 = par_t[:, l2 % 2]
            P.add("sp", lambda e: e.dma_start(out=par2.rearrange("p g c -> p (g c)"), in_=par_d[li2]),
                  writes=[R_par[l2 % 2]], dma=True)
            P.add("sp", lambda e: e.dma_start(out=w4_t[:, l2 % 2, :], in_=w4_d[li2]), writes=[R_w4[l2 % 2]], dma=True)
            early_done.add(l2)

        def emit_diag(l2, g):
            P.add("dve", lambda e: e.tensor_tensor(
                out=Lm_t[:].rearrange("p s j c -> p (s j) c"),
                in0=dm_t[:].unsqueeze(1).to_broadcast([128, 32, 32]),
                in1=w4_t[:, l2 % 2, 32 * g:32 * g + 32].unsqueeze(2).to_broadcast([128, 32, 32]), op=ALU.mult),
                reads=[R_dm, R_w4[l2 % 2]], writes=[R_dg])

        for lbi, (blk, li) in enumerate(LB):
            last_layer = (li == NL - 1)
            skip0 = (blk >= 1)
            noy0 = last_layer or skip0
            save_tails = (blk + 1 < NBLK)
            par = par_t[:, lbi % 2]
            Rpar = R_par[lbi % 2]
            pw = pw_t[:, lbi % 2]
            Rpw = R_pw[lbi % 2]

            def pcol(g, c, par=par):
                return par[:, g, c:c + 1]

            if lbi not in early_done:
                emit_par_load(lbi)
            P.add("sp", lambda e, li=li: e.dma_start(out=lnbc_t[:].rearrange("p a d -> p (a d)"), in_=lnbc_d[li]),
                  writes=[R_lnbc], dma=True)

            if li == 0 and blk == 0:
                for tc in range(9):
                    p0_load_f32(blk, tc)
                for tc in range(3):
                    p0_load_bf16(blk, tc)
                for tc in range(9):
                    p0_transpose(blk, tc)
                    if tc + 3 < 9:
                        p0_load_bf16(blk, tc + 3)

            if debug:
                o = P.add("sp", lambda e, lbi=lbi: e.dma_start(out=dbg_x[lbi], in_=xT_t[:].rearrange("p c t -> p (c t)")),
                          reads=R_xT, dma=True)
                out_dmas.append(o.dma_id)
            sz = arena(0, 9, BF16, 4 * TBH)
            cb = arena(9, 9, BF16, 4 * TBH)
            ubs = [arena(18, 3, BF16, 32 + TBH), arena(21, 3, BF16, 32 + TBH)]

            def B_DIAG(g):
                if g == 0 and lbi in diag0_done:
                    return
                emit_diag(lbi, g)

            def B_W(g, t):
                off, N = TILES[t]
                ub = ubs[g % 2]
                if skip0 and t == 0:
                    P.add("pool", lambda e, ub=ub: e.memset(ub.ap[:, 0:96], 0.0), writes=ub.res)
                    P.add("pool", lambda e, ub=ub, g=g, li=li: e.tensor_copy(out=ub.ap[:, 66:96], in_=tB_t[:, li, g, :]),
                          reads=[R_tB[li][g]], writes=ub.res)
                    pump_group(lbi, 3 * g)
                    return
                if t == 0:
                    P.add("pool", lambda e, ub=ub: e.memset(ub.ap[:, 0:32], 0.0), writes=ub.res)
                    pump_group(lbi, 3 * g)
                xres = [R_xT[c] for c in TILE_CHUNKS[t]]
                banks = []
                for c3 in range(3):
                    ci = 3 * g + c3
                    jidx, slot = win_job[(lbi, ci)]
                    if noy0 and t == 0 and c3 == 2:
                        banks.append(None)
                        continue
                    bk = new_bank()
                    banks.append(bk)
                    for k in range(8):
                        P.add("pe", lambda e, bk=bk, slot=slot, k=k, off=off, N=N: e.matmul(
                            bk.ap[:, 0:N], lhsT=wrg_t[:, slot, k, :], rhs=xT_t[:, k, off:off + N],
                            start=(k == 0), stop=(k == 7)),
                            reads=[R_wrg[slot]] + xres, writes=bk.res if k in (0, 7) else [])
                Pv, Pg, Pz = banks
                par2 = (g * 3 + t) % 2
                sg = arena(24, 2, F32, 512)
                P.add("act", lambda e, sg=sg, Pg=Pg, N=N: e.activation(out=sg.ap[:, 0:N], in_=Pg.ap[:, 0:N], func=AF.Tanh,
                                                                       scale=0.5),
                      reads=Pg.res, writes=sg.res)
                P.add("dve", lambda e, ub=ub, Pv=Pv, sg=sg, off=off, N=N: e.scalar_tensor_tensor(
                    out=ub.ap[:, 32 + off:32 + off + N], in0=sg.ap[:, 0:N], scalar=1.0, in1=Pv.ap[:, 0:N],
                    op0=ALU.add, op1=ALU.mult),
                    reads=Pv.res + sg.res, writes=ub.res)
                if save_tails and t == 2:
                    P.add("pool", lambda e, ub=ub, g=g, li=li: e.tensor_copy(out=tB_t[:, li, g, :], in_=ub.ap[:, 1090:1120]),
                          reads=ub.res, writes=[R_tB[li][g]])
                if noy0 and t == 0:
                    return
                P.add("act", lambda e, sz=sz, Pz=Pz, off=off, N=N, g=g: e.activation(
                    out=sz.ap[:, g * TBH + off:g * TBH + off + N], in_=Pz.ap[:, 0:N], func=AF.Silu),
                    reads=Pz.res, writes=sz.res)

            def B_COMBINE():
                pass

            def B_REP(g, t):
                off, N = TILES[t]
                ub = ubs[g % 2]
                for s4 in range(4):
                    bk = new_bank()
                    for i4 in range(4):
                        P.add("pe", lambda e, bk=bk, ub=ub, s4=s4, i4=i4, off=off, N=N: e.matmul(
                            bk.ap[32 * i4:32 * i4 + 32, 0:N], lhsT=idb_t[:, 32 * s4:32 * s4 + 32],
                            rhs=ub.ap[:, 29 + off + i4:29 + off + i4 + N], start=True, stop=True, tile_position=(0, 32 * i4)),
                            reads=[R_idb] + ub.res, writes=bk.res if i4 in (0, 3) else [])
                    P.add("act", lambda e, bk=bk, s4=s4, off=off, N=N: e.activation(
                        out=U_t[:, s4, 29 + off:29 + off + N], in_=bk.ap[:, 0:N], func=AF.Copy),
                        reads=bk.res, writes=[R_U[t]])

            def B_CONV(g, t):
                off, N = TILES[t]
                if noy0 and t == 0:
                    return
                Pc = new_bank()
                ures = [R_U[t]] + ([R_U[t - 1]] if t > 0 else [])
                for j in range(8):
                    K = 128
                    for s4 in range(4):
                        first = (j == 0 and s4 == 0)
                        last = (j == 7 and s4 == 3)
                        P.add("pe", lambda e, Pc=Pc, j=j, s4=s4, K=K, off=off, N=N: e.matmul(
                            Pc.ap[32 * s4:32 * s4 + 32, 0:N], lhsT=Lm_t[0:K, s4, j, :],
                            rhs=U_t[0:K, s4, off + 1 + 4 * j:off + 1 + 4 * j + N],
                            start=(j == 0), stop=(j == 7), tile_position=(0, 32 * s4)),
                            reads=[R_dg] + ures, writes=Pc.res if (first or last) else [])
                P.add("act", lambda e, cb=cb, Pc=Pc, off=off, N=N, g=g, pc=pcol: e.activation(
                    out=cb.ap[:, g * TBH + off:g * TBH + off + N], in_=Pc.ap[:, 0:N], func=AF.Identity,
                    scale=0.5, bias=pc(g, 4)), reads=Pc.res + [Rpar], writes=cb.res)

            def B_FINAL_stages(t, bs):
                off, N = TILES[t]
                m = arena(28 + 6 * bs, 2, F32, 512)
                X = arena(30 + 6 * bs, 2, F32, 512)
                rstd = arena(32 + 6 * bs, 2, F32, 512)
                bank = {}

                def s_stats():
                    Psum = bank["sum"] = new_bank()
                    Psq = bank["sq"] = new_bank()
                    for g in range(4):
                        P.add("pe", lambda e, g=g: e.matmul(
                            Psum.ap[:, 0:N], lhsT=ones_t[:], rhs=cb.ap[:, g * TBH + off:g * TBH + off + N],
                            start=(g == 0), stop=(g == 3)), reads=[R_ones] + cb.res, writes=Psum.res if g in (0, 3) else [])
                    for g in range(4):
                        sqt = arena(26 + (g % 2), 1, BF16, 512)
                        P.add("act", lambda e, sqt=sqt, g=g: e.activation(
                            out=sqt.ap[:, 0:N], in_=cb.ap[:, g * TBH + off:g * TBH + off + N], func=AF.Square),
                            reads=cb.res, writes=sqt.res)
                        P.add("pe", lambda e, sqt=sqt, g=g: e.matmul(
                            Psq.ap[:, 0:N], lhsT=ones_t[:], rhs=sqt.ap[:, 0:N], start=(g == 0), stop=(g == 3)),
                            reads=[R_ones] + sqt.res, writes=Psq.res if g in (0, 3) else [])

                def s_var():
                    Psum, Psq = bank["sum"], bank["sq"]
                    P.add("dve", lambda e: e.tensor_scalar(out=m.ap[:, 0:N], in0=Psum.ap[:, 0:N],
                                                            scalar1=1.0 / 512, scalar2=None, op0=ALU.mult),
                          reads=Psum.res, writes=m.res)
                    P.add("dve", lambda e: e.tensor_tensor(out=X.ap[:, 0:N], in0=m.ap[:, 0:N], in1=m.ap[:, 0:N],
                                                            op=ALU.mult), reads=m.res, writes=X.res)
                    P.add("dve", lambda e: e.scalar_tensor_tensor(
                        out=X.ap[:, 0:N], in0=Psq.ap[:, 0:N], scalar=1.0 / 512, in1=X.ap[:, 0:N], op0=ALU.mult,
                        op1=ALU.subtract), reads=Psq.res + X.res, writes=X.res)

                def s_rstd():
                    P.add("act", lambda e: e.activation(out=X.ap[:, 0:N], in_=X.ap[:, 0:N], func=AF.Sqrt,
                                                        bias=eps_t[:, 0:1]),
                          reads=X.res + [R_eps], writes=X.res)
                    P.add("dve", lambda e: e.reciprocal(out=rstd.ap[:, 0:N], in_=X.ap[:, 0:N]),
                          reads=X.res, writes=rstd.res)

                def s_norm(g):
                    P.add("dve" if bs == 0 else "pool", lambda e, g=g: e.tensor_tensor(
                        out=X.ap[:, 0:N], in0=cb.ap[:, g * TBH + off:g * TBH + off + N], in1=m.ap[:, 0:N], op=ALU.subtract),
                        reads=cb.res + m.res, writes=X.res)
                    P.add("dve", lambda e: e.tensor_tensor(
                        out=X.ap[:, 0:N], in0=X.ap[:, 0:N], in1=rstd.ap[:, 0:N], op=ALU.mult),
                        reads=X.res + rstd.res, writes=X.res)

                def s_gate(g):
                    P.add("act", lambda e, g=g, pc=pcol: e.activation(
                        out=X.ap[:, 0:N], in_=X.ap[:, 0:N], func=AF.Silu, scale=pc(g, 5), bias=pc(g, 6)),
                        reads=X.res + [Rpar], writes=X.res)
                    P.add("pool", lambda e, g=g: e.tensor_tensor(
                        out=yb_t[:, 4 + g, off:off + N], in0=X.ap[:, 0:N], in1=sz.ap[:, g * TBH + off:g * TBH + off + N],
                        op=ALU.mult), reads=X.res + sz.res, writes=[R_y[4 + g][t]])

                st = [lambda: (s_stats(), s_var()), s_rstd, lambda: s_norm(0)]
                for g in range(4):
                    st.append(lambda g=g: (s_gate(g), s_norm(g + 1) if g < 3 else None))
                return st

            fin_wait = []
            fin_active = {}

            def fin_tick():
                for bs in (0, 1):
                    if bs not in fin_active and fin_wait:
                        fin_active[bs] = B_FINAL_stages(fin_wait.pop(0), bs)
                for bs in (0, 1):
                    st = fin_active.get(bs)
                    if st:
                        st.pop(0)()
                    if bs in fin_active and not fin_active[bs]:
                        del fin_active[bs]

            def B_FINAL(t):
                if noy0 and t == 0:
                    return
                fin_wait.append(t)

            B_DIAG(0)
            steps = [(g, t) for g in range(4) for t in range(3)]

            def after_conv(gt):
                if gt[0] == 3:
                    B_FINAL(gt[1])
                if gt[1] == 2 and gt[0] < 3:
                    B_DIAG(gt[0] + 1)

            for n, (g, t) in enumerate(steps):
                if g == 0 and t >= 1 and carry_tr:
                    for _ in range(1 if t == 1 and len(carry_tr) > 1 else len(carry_tr)):
                        ptc, ppp, pblk = carry_tr.pop(0)
                        transposes_to_xT(p2b_t[:, ppp, :], [R_p2b[ppp]], ptc, pblk)
                B_W(g, t)
                pump_tick(2)
                fin_tick()
                if n >= 1:
                    B_REP(*steps[n - 1])
                if n >= 2:
                    B_CONV(*steps[n - 2])
                    after_conv(steps[n - 2])
            B_REP(*steps[-1])
            B_CONV(*steps[-2])
            after_conv(steps[-2])
            fin_tick()
            B_CONV(*steps[-1])
            after_conv(steps[-1])

            def A_PREP(g):
                ua = arena(40 + 5 * (g % 2), 5, F32, 2 + TBH)
                if skip0:
                    P.add("pool", lambda e, ua=ua: e.memset(ua.ap[:, 0:66], 0.0), writes=ua.res)
                    P.add("pool", lambda e, ua=ua, g=g, li=li: e.tensor_copy(out=ua.ap[:, 64:66], in_=tA_t[:, li, g, :]),
                          reads=[R_tA[li][g]], writes=ua.res)
                else:
                    P.add("pool", lambda e, ua=ua: e.memset(ua.ap[:, 0:2], 0.0), writes=ua.res)

            A_PREP(0)
            for g in range(4):
                gp = g % 2
                ua = arena(40 + 5 * gp, 5, F32, 2 + TBH)
                if g + 1 < 4:
                    A_PREP(g + 1)
                pump_group(lbi, 12 + 4 * g)
                for t, (off, N) in enumerate(TILES):
                    if skip0 and t == 0:
                        B_COMBINE()
                        fin_tick()
                        continue
                    xres = [R_xT[c] for c in TILE_CHUNKS[t]]
                    banks = []
                    for c4 in range(4):
                        ci = 12 + 4 * g + c4
                        jidx, slot = win_job[(lbi, ci)]
                        if noy0 and t == 0 and c4 in (0, 3):
                            banks.append(None)
                            continue
                        bk = new_bank()
                        banks.append(bk)
                        for k in range(8):
                            P.add("pe", lambda e, bk=bk, slot=slot, k=k, off=off, N=N: e.matmul(
                                bk.ap[:, 0:N], lhsT=wrg_t[:, slot, k, :], rhs=xT_t[:, k, off:off + N],
                                start=(k == 0), stop=(k == 7)),
                                reads=[R_wrg[slot]] + xres, writes=bk.res if k in (0, 7) else [])
                    Pbg, Pcg, Pv, Pz = banks
                    par2 = (g * 3 + t) % 2
                    t1 = arena((50, 20)[par2], 2, F32, 512)
                    cA = arena((52, 22)[par2], 2, F32, 512)
                    sA = arena((54, 24)[par2], 2, F32, 512)
                    y1 = arena(18, 2, F32, 512)
                    P.add("act", lambda e, t1=t1, Pv=Pv, N=N: e.activation(out=t1.ap[:, 0:N], in_=Pv.ap[:, 0:N], func=AF.Copy),
                          reads=Pv.res, writes=t1.res)
                    P.add("dve", lambda e, ua=ua, Pcg=Pcg, t1=t1, off=off, N=N: e.tensor_tensor(
                        out=ua.ap[:, 2 + off:2 + off + N], in0=Pcg.ap[:, 0:N], in1=t1.ap[:, 0:N], op=ALU.mult),
                        reads=Pcg.res + t1.res, writes=ua.res)
                    if save_tails and t == 2:
                        P.add("pool", lambda e, ua=ua, g=g, li=li: e.tensor_copy(out=tA_t[:, li, g, :], in_=ua.ap[:, 1088:1090]),
                              reads=ua.res, writes=[R_tA[li][g]])
                    if not (noy0 and t == 0):
                        P.add("act", lambda e, sA=sA, Pz=Pz, N=N: e.activation(out=sA.ap[:, 0:N], in_=Pz.ap[:, 0:N], func=AF.Silu),
                              reads=Pz.res, writes=sA.res)
                        P.add("dve", lambda e, Pbg=Pbg, sA=sA, N=N: e.tensor_tensor(
                            out=sA.ap[:, 0:N], in0=Pbg.ap[:, 0:N], in1=sA.ap[:, 0:N], op=ALU.mult),
                            reads=Pbg.res + sA.res, writes=sA.res)
                    if not (noy0 and t == 0):
                        P.add("act", lambda e, cA=cA, ua=ua, off=off, N=N, g=g, pc=pcol: e.activation(
                            out=cA.ap[:, 0:N], in_=ua.ap[:, off:off + N], func=AF.Identity, scale=pc(g, 0), bias=pc(g, 3)),
                            reads=ua.res + [Rpar], writes=cA.res)
                        for k in (1, 2):
                            P.add("dve", lambda e, cA=cA, ua=ua, off=off, N=N, g=g, k=k, pc=pcol: e.scalar_tensor_tensor(
                                out=cA.ap[:, 0:N], in0=ua.ap[:, off + k:off + k + N], scalar=pc(g, k), in1=cA.ap[:, 0:N],
                                op0=ALU.mult, op1=ALU.add), reads=ua.res + [Rpar] + cA.res, writes=cA.res)
                        P.add("dve" if par2 == 0 else "pool", lambda e, sA=sA, cA=cA, off=off, N=N, g=g: e.tensor_tensor(
                            out=yb_t[:, g, off:off + N], in0=sA.ap[:, 0:N], in1=cA.ap[:, 0:N], op=ALU.mult),
                            reads=sA.res + cA.res, writes=[R_y[g][t]])
                    B_COMBINE()
                    pump_tick(3)
                    fin_tick()

            pump_group(lbi, 28)
            while fin_active or fin_wait:
                fin_tick()
            cus = [arena(0, 5, F32, 16 + TBH), arena(5, 5, F32, 16 + TBH)]
            sCs = [arena(31, 3, BF16, TBH), arena(34, 3, BF16, TBH)]
            lv = [arena(10, 5, F32, 16 + TBH), arena(26, 5, F32, 16 + TBH)]
            dpls = [arena(37, 3, BF16, TBH), arena(37, 3, BF16, TBH)]

            def C_W(g):
                cu, sC = cus[CG.index(g) % 2], sCs[CG.index(g) % 2]
                if skip0:
                    if CG.index(g) < 2:
                        P.add("pool", lambda e, cu=cu: e.memset(cu.ap[:, 0:80], 0.0), writes=cu.res)
                    P.add("pool", lambda e, cu=cu, g=g, li=li: e.tensor_copy(out=cu.ap[:, 65:80], in_=tC_t[:, li, g, :]),
                          reads=[R_tC[li][g]], writes=cu.res)
                elif CG.index(g) < 2:
                    P.add("pool", lambda e, cu=cu: e.memset(cu.ap[:, 0:16], 0.0), writes=cu.res)
                pump_group(lbi, 28 + 2 * CG.index(g))
                for t, (off, N) in enumerate(TILES):
                    if skip0 and t == 0:
                        continue
                    xres = [R_xT[c] for c in TILE_CHUNKS[t]]
                    banks = []
                    for c2 in range(2):
                        ci = 28 + 2 * CG.index(g) + c2
                        jidx, slot = win_job[(lbi, ci)]
                        if noy0 and t == 0 and c2 == 1:
                            banks.append(None)
                            continue
                        bk = new_bank()
                        banks.append(bk)
                        for k in range(8):
                            P.add("pe", lambda e, bk=bk, slot=slot, k=k, off=off, N=N: e.matmul(
                                bk.ap[:, 0:N], lhsT=wrg_t[:, slot, k, :], rhs=xT_t[:, k, off:off + N],
                                start=(k == 0), stop=(k == 7)),
                                reads=[R_wrg[slot]] + xres, writes=bk.res if k in (0, 7) else [])
                    Pu, Pz = banks
                    P.add("act", lambda e, cu=cu, Pu=Pu, off=off, N=N: e.activation(
                        out=cu.ap[:, 16 + off:16 + off + N], in_=Pu.ap[:, 0:N], func=AF.Copy), reads=Pu.res, writes=cu.res)
                    if save_tails and t == 2:
                        P.add("pool", lambda e, cu=cu, g=g, li=li: e.tensor_copy(out=tC_t[:, li, g, :], in_=cu.ap[:, 1089:1104]),
                              reads=cu.res, writes=[R_tC[li][g]])
                    if noy0 and t == 0:
                        continue
                    P.add("act", lambda e, sC=sC, Pz=Pz, off=off, N=N: e.activation(
                        out=sC.ap[:, off:off + N], in_=Pz.ap[:, 0:N], func=AF.Silu), reads=Pz.res, writes=sC.res)

            def C_CHAIN(g, i):
                cu, dpl = cus[i % 2], dpls[i % 2]
                if i == 0:
                    for l2 in lv:
                        P.add("pool", lambda e, l2=l2: e.memset(l2.ap[:, 0:16], 0.0), writes=l2.res)
                src = cu
                sh = 1
                for lev in range(g + 1):
                    dst = lv[lev % 2]
                    P.add("dve", lambda e, src=src, dst=dst, sh=sh: e.tensor_tensor(
                        out=dst.ap[:, 16:16 + TBH], in0=src.ap[:, 16:16 + TBH], in1=src.ap[:, 16 - sh:16 - sh + TBH], op=ALU.add),
                        reads=src.res, writes=dst.res)
                    src = dst
                    sh *= 2
                sw = src
                P.add("dve", lambda e, sw=sw, g=g, blk=blk: e.tensor_tensor(
                    out=sw.ap[:, 16 + HALO:16 + HALO + 16], in0=sw.ap[:, 16 + HALO:16 + HALO + 16], in1=corr_t[:, blk, g, :], op=ALU.mult),
                    reads=sw.res + [R_corr], writes=sw.res)
                P.add("dve", lambda e, sw=sw, cu=cu, dpl=dpl, g=g: e.scalar_tensor_tensor(
                    out=dpl.ap[:, 0:TBH], in0=sw.ap[:, 16:16 + TBH], scalar=1.0 / POOL_W[g], in1=cu.ap[:, 16:16 + TBH],
                    op0=ALU.mult, op1=ALU.subtract), reads=sw.res + cu.res, writes=dpl.res)

            def C_POOL(g):
                sC, dpl = sCs[CG.index(g) % 2], dpls[CG.index(g) % 2]
                for t, (off, N) in enumerate(TILES):
                    if noy0 and t == 0:
                        continue
                    Pp = new_bank()
                    P.add("pe", lambda e, Pp=Pp, dpl=dpl, g=g, off=off, N=N, pw=pw: e.matmul(
                        Pp.ap[:, 0:N], lhsT=pw[:, g, :], rhs=dpl.ap[:, off:off + N], start=True, stop=True),
                        reads=[Rpw] + dpl.res, writes=Pp.res)
                    q = arena(15, 2, F32, 512)
                    P.add("dve", lambda e, q=q, Pp=Pp, g=g, N=N, pc=pcol: e.tensor_scalar(
                        out=q.ap[:, 0:N], in0=Pp.ap[:, 0:N], scalar1=pc(g, 7), scalar2=pc(g, 8), op0=ALU.add, op1=ALU.mult),
                        reads=Pp.res + [Rpar], writes=q.res)
                    P.add("pool", lambda e, q=q, sC=sC, g=g, off=off, N=N: e.tensor_tensor(
                        out=yb_t[:, 8 + g, off:off + N], in0=q.ap[:, 0:N], in1=sC.ap[:, off:off + N], op=ALU.mult),
                        reads=q.res + sC.res, writes=[R_y[8 + g][t]])

            EC_ORDER = list(range(8)) + [8 + CG[0], 8 + CG[1], 8 + CG[2], 8 + CG[3]]
            p2_banks = {}

            def p2_mm(tc, lo, hi):
                q0, n = CHUNKS[tc]
                t = 0 if tc == 0 else (1 if tc <= 4 else 2)
                if tc not in p2_banks:
                    p2_banks[tc] = [new_bank(), new_bank()]
                Po = p2_banks[tc]
                for h in range(2):
                    for idx in range(lo, hi):
                        ec = EC_ORDER[idx]
                        P.add("pe", lambda e, Po=Po, h=h, ec=ec, idx=idx, q0=q0, n=n: e.matmul(
                            Po[h].ap[0:n, :], lhsT=yb_t[:, ec, q0:q0 + n], rhs=wo_t[:, ec, h * 512:(h + 1) * 512],
                            start=(idx == 0), stop=(idx == 11)),
                            reads=[R_y[ec][t], R_wo[ec]], writes=Po[h].res if idx in (0, 11) else [])
                return Po

            p2_first = [1, 2] if noy0 else [0, 1]
            C_W(CG[0])
            for i, g in enumerate(CG):
                if i + 1 < 4:
                    C_W(CG[i + 1])
                C_CHAIN(g, i)
                if i == 3:
                    for tc in p2_first:
                        p2_mm(tc, 0, 10)
                C_POOL(g)

            if lbi + 1 < len(LB):
                pump_group(lbi + 1, 0)
            if debug:
                o = P.add("sp", lambda e, lbi=lbi: e.dma_start(out=dbg_y[lbi], in_=yb_t[:].rearrange("p c t -> p (c t)")),
                          reads=[r for rr in R_y for r in rr], dma=True)
                out_dmas.append(o.dma_id)
            tr_pending = []
            back_pending = []
            nx_todo = list(range(1, 9)) if (last_layer and blk + 1 < NBLK) else []
            nx_loaded = []
            for tc in range(9):
                if noy0 and tc == 0:
                    continue
                q0, n = CHUNKS[tc]
                t = 0 if tc == 0 else (1 if tc <= 4 else 2)
                Po = p2_mm(tc, 10, 12) if tc in p2_banks else p2_mm(tc, 0, 12)
                if lbi + 1 < len(LB):
                    if tc == 3:
                        emit_par_load(lbi + 1)
                    if tc == 6:
                        emit_diag(lbi + 1, 0)
                        diag0_done.add(lbi + 1)
                if len(tr_pending) == 2:
                    ptc, ppp = tr_pending.pop(0)
                    transposes_to_xT(p2b_t[:, ppp, :], [R_p2b[ppp]], ptc, blk)
                if last_layer and blk + 1 < NBLK:
                    if nx_loaded:
                        p0_transpose(blk + 1, nx_loaded.pop(0))
                    for _ in range(2 if tc == 1 else 1):
                        if nx_todo:
                            c = nx_todo.pop(0)
                            p0_load_bf16(blk + 1, c)
                            nx_loaded.append(c)
                    if tc == 1 and nx_loaded:
                        pass
                pp = p2ctr[0] % 2
                p2ctr[0] += 1
                Rt = Rb_t[0:n, tc, :]
                for h in range(2):
                    P.add("dve", lambda e, Rt=Rt, Po=Po, h=h, n=n: e.scalar_tensor_tensor(
                        out=Rt[:, h * 512:(h + 1) * 512], in0=Rt[:, h * 512:(h + 1) * 512], scalar=ALPHA,
                        in1=Po[h].ap[0:n, :], op0=ALU.mult, op1=ALU.add), reads=[R_Rb[tc]] + Po[h].res, writes=[R_Rb[tc]])
                for h in range(2):
                    P.add("dve", lambda e, Rt=Rt, h=h, n=n, pp=pp: e.bn_stats(out=st_t[0:n, pp, h, :],
                                                                               in_=Rt[:, h * 512:(h + 1) * 512]),
                          reads=[R_Rb[tc]], writes=[R_st[pp]])
                P.add("dve", lambda e, n=n, pp=pp: e.bn_aggr(out=mv_t[0:n, pp, 0:2],
                                                             in_=st_t[0:n, pp].rearrange("p a s -> p (a s)")),
                      reads=[R_st[pp]], writes=[R_mv[pp]])
                P.add("act", lambda e, n=n, pp=pp: e.activation(out=sd_t[0:n, pp, 1:2], in_=mv_t[0:n, pp, 1:2], func=AF.Sqrt,
                                                                bias=eps_t[0:n, 0:1]),
                      reads=[R_mv[pp], R_eps], writes=[R_sd[pp]])
                P.add("dve", lambda e, n=n, pp=pp: e.reciprocal(out=mv_t[0:n, pp, 2:3], in_=sd_t[0:n, pp, 1:2]),
                      reads=[R_sd[pp], R_mv[pp]], writes=[R_mv[pp]])
                P.add("dve", lambda e, n=n, pp=pp: e.tensor_scalar(out=mv_t[0:n, pp, 3:4], in0=mv_t[0:n, pp, 0:1],
                                                                   scalar1=mv_t[0:n, pp, 2:3], scalar2=-1.0, op0=ALU.mult, op1=ALU.mult),
                      reads=[R_mv[pp]], writes=[R_mv[pp]])
                xnb = arena(4 * pp, 4, F32, 1024)
                xn = xnb.ap[0:n, :]
                P.add("act", lambda e, xn=xn, Rt=Rt, n=n, pp=pp: e.activation(out=xn, in_=Rt, func=AF.Identity,
                                                                               scale=mv_t[0:n, pp, 2:3], bias=mv_t[0:n, pp, 3:4]),
                      reads=[R_Rb[tc], R_mv[pp]], writes=xnb.res)
                def back(xn=xn, xnb=xnb, Rt=Rt, n=n, q0=q0, tc=tc, pp=pp, blk=blk, last_layer=last_layer):
                    P.add("pool", lambda e: e.tensor_tensor(out=xn, in0=xn, in1=lnbc_t[0:n, 0, :], op=ALU.mult),
                          reads=xnb.res + [R_lnbc], writes=xnb.res)
                    P.add("pool", lambda e: e.tensor_tensor(out=Rt, in0=xn, in1=lnbc_t[0:n, 1, :], op=ALU.add),
                          reads=xnb.res + [R_lnbc], writes=[R_Rb[tc]])
                    if last_layer:
                        o = P.add("sp", lambda e: e.dma_start(out=out_d[blk, q0 - HALO:q0 - HALO + n, :], in_=Rt),
                                  reads=[R_Rb[tc]], dma=True)
                        out_dmas.append(o.dma_id)
                    else:
                        P.add("act", lambda e: e.activation(out=p2b_t[0:n, pp, :], in_=Rt, func=AF.Copy),
                              reads=[R_Rb[tc]], writes=[R_p2b[pp]])
                        tr_pending.append((tc, pp))
                while back_pending:
                    back_pending.pop(0)()
                back_pending.append(back)
            if len(tr_pending) == 2:
                ptc, ppp = tr_pending.pop(0)
                transposes_to_xT(p2b_t[:, ppp, :], [R_p2b[ppp]], ptc, blk)
            while back_pending:
                back_pending.pop(0)()
            for ptc, ppp in tr_pending:
                if CHUNK_TILE[ptc] == 2:
                    carry_tr.append((ptc, ppp, blk))
                else:
                    transposes_to_xT(p2b_t[:, ppp, :], [R_p2b[ppp]], ptc, blk)
            if last_layer and blk + 1 < NBLK:
                while nx_loaded or nx_todo:
                    if nx_loaded:
                        c = nx_loaded.pop(0)
                        if CHUNK_TILE[c] == 2 and not nx_todo:
                            carry_tr.append((c, p0_pp[(blk + 1, c)], blk + 1))
                        else:
                            p0_transpose(blk + 1, c)
                    if nx_todo:
                        c = nx_todo.pop(0)
                        p0_load_bf16(blk + 1, c)
                        nx_loaded.append(c)
                for c in range(1, 9):
                    p0_load_f32(blk + 1, c)

        P.emit(nc, final_wait_dmas=out_dmas)
    return nc


def _prep_weights(layers, w_in, conv_a_w, conv_a_b, conv_b_w, conv_b_b, ln_b_g, ln_b_b,
                  pool_w, pool_b, pool_scale, w_out, ln_g, ln_b):
    NL = len(layers)
    w_in_r = np.empty((NL, 36, 128, 1024), np.float32)
    w_out_r = np.empty((NL, 12, 128, 1024), np.float32)
    pool_w_r = np.empty((NL, 128, 512), np.float32)
    params = np.empty((NL, 128, 4, NPAR), np.float32)
    lnbc = np.empty((NL, 128, 2048), np.float32)
    for i, l in enumerate(layers):
        wi = np.asarray(w_in[l], np.float32).reshape(8, 128, D_IN)
        for ci, base in enumerate(COLS):
            w_in_r[i, ci] = wi[:, :, base:base + 128].transpose(1, 0, 2).reshape(128, 1024)
        w_out_r[i] = np.asarray(w_out[l], np.float32).reshape(12, 128, 1024)
        pool_w_r[i] = np.asarray(pool_w[l], np.float32).transpose(1, 0, 2).reshape(128, 512)
        caw = np.asarray(conv_a_w[l], np.float32).reshape(3, 4, 128)
        cbw = np.asarray(conv_b_w[l], np.float32).reshape(31, 4, 128)
        params[i, :, :, 0:3] = caw.transpose(2, 1, 0)
        params[i, :, :, 3] = np.asarray(conv_a_b[l], np.float32).reshape(4, 128).T
        params[i, :, :, 4] = np.asarray(conv_b_b[l], np.float32).reshape(4, 128).T
        params[i, :, :, 5] = np.asarray(ln_b_g[l], np.float32).reshape(4, 128).T
        params[i, :, :, 6] = np.asarray(ln_b_b[l], np.float32).reshape(4, 128).T
        params[i, :, :, 7] = np.asarray(pool_b[l], np.float32).reshape(4, 128).T
        params[i, :, :, 8] = np.asarray(pool_scale[l], np.float32).reshape(4, 128).T
        lnbc[i, :, 0:1024] = np.asarray(ln_g[l], np.float32)[None, :]
        lnbc[i, :, 1024:2048] = np.asarray(ln_b[l], np.float32)[None, :]
    w4 = np.zeros((NL, 128, 4, 4, 8), np.float32)
    for i, l in enumerate(layers):
        cbw = np.zeros((32, 512), np.float32)
        cbw[1:32] = np.asarray(conv_b_w[l], np.float32)
        w4[i] = cbw.reshape(8, 4, 4, 4, 32).transpose(1, 4, 2, 3, 0).reshape(128, 4, 4, 8)
    dmask = np.tile(np.eye(32, dtype=np.float32), (4, 1))
    return dict(w4=w4.reshape(NL, 128, 128), dmask=dmask, w_in_r=w_in_r, w_out_r=w_out_r, pool_w_r=pool_w_r,
                params=params.reshape(NL, 128, 4 * NPAR), lnbc=lnbc,
                ident=np.eye(128, dtype=np.float32))


def _shard_x(x):
    shards = []
    for c in range(NCORES):
        b = c // (NCORES // BATCH)
        s0 = (c % (NCORES // BATCH)) * TPC
        xin = np.zeros((NBLK, TBH, D_MODEL), np.float32)
        aux = np.zeros((128, 2 + 32), np.float32)
        for j in range(NBLK):
            start = s0 + j * TB
            lo = max(0, start - HALO)
            xin[j, HALO - (start - lo):] = x[b, lo:start + TB]
            aux[:, j] = 0.0 if start == 0 else 1.0
            aux[:, 2 + 16 * j:2 + 16 * (j + 1)] = (start + np.arange(16, dtype=np.float32))[None, :]
        shards.append((xin, aux))
    return shards


def _gather(res):
    out = np.empty((BATCH, SEQ, D_MODEL), np.float32)
    for c in range(NCORES):
        b = c // (NCORES // BATCH)
        s0 = (c % (NCORES // BATCH)) * TPC
        out[b, s0:s0 + TPC] = np.asarray(res[c]["out"], np.float32).reshape(TPC, D_MODEL)
    return out


_NC_CACHE = {}


def _run(layers, x, weights):
    key = tuple(layers)
    if key not in _NC_CACHE:
        _NC_CACHE[key] = build_nc(list(layers))
    nc = _NC_CACHE[key]
    wts = _prep_weights(list(layers), **weights)
    shards = _shard_x(np.asarray(x, np.float32))
    in_maps = []
    for c in range(NCORES):
        m = dict(wts)
        m["xin"], m["aux"] = shards[c]
        in_maps.append(m)
    res = run_bass_kernel_spmd(nc, in_maps, core_ids=list(range(NCORES)))
    return _gather(res.results)


FUSED = True


def kernel(x, w_in, conv_a_w, conv_a_b, conv_b_w, conv_b_b, ln_b_g, ln_b_b,
           pool_w, pool_b, pool_scale, w_out, ln_g, ln_b):
    weights = dict(w_in=w_in, conv_a_w=conv_a_w, conv_a_b=conv_a_b, conv_b_w=conv_b_w, conv_b_b=conv_b_b,
                   ln_b_g=ln_b_g, ln_b_b=ln_b_b, pool_w=pool_w, pool_b=pool_b, pool_scale=pool_scale,
                   w_out=w_out, ln_g=ln_g, ln_b=ln_b)
    weights = {k: np.asarray(v) for k, v in weights.items()}
    x = np.asarray(x, np.float32)
    if FUSED:
        return _run((0, 1), x, weights)
    for l in range(DEPTH):
        x = _run((l,), x, weights)
    return x
```

```python
import contextlib
import numpy as np
import concourse.bass as bass
import concourse.mybir as mybir
from concourse.bass_utils import run_bass_kernel_spmd

F32 = mybir.dt.float32
BF16 = mybir.dt.bfloat16
AF = mybir.ActivationFunctionType
ALU = mybir.AluOpType

D_MODEL = 1024
BATCH = 2
SEQ = 8192
DEPTH = 2
D_IN = 4608
D_MIX = 1536
NCORES = 8
TPC = BATCH * SEQ // NCORES
NBLK = 2
TB = TPC // NBLK
HALO = 64
TBH = TB + HALO
ALPHA = (2.0 * DEPTH) ** 0.25
LN_EPS = 1e-5
POOL_W = (2, 4, 8, 16)
NPAR = 9

TILES = [(0, 64), (64, 512), (576, 512)]
CHUNKS = [(0, 64)] + [(64 + 128 * i, 128) for i in range(8)]
TILE_CHUNKS = [[0], [1, 2, 3, 4], [5, 6, 7, 8]]
CHUNK_TILE = [0, 1, 1, 1, 1, 2, 2, 2, 2]

COLS = []
for g in range(4):
    COLS += [2048 + 128 * g, 2560 + 128 * g, 3072 + 128 * g]
for g in range(4):
    COLS += [0 + 128 * g, 512 + 128 * g, 1024 + 128 * g, 1536 + 128 * g]
CG = (3, 2, 1, 0)
for g in CG:
    COLS += [3584 + 128 * g, 4096 + 128 * g]

ENGS = ("pe", "act", "dve", "pool", "sp")
NDMASEM = 24


class Res:
    __slots__ = ("name", "w", "r")

    def __init__(self, name):
        self.name = name
        self.w = None
        self.r = {}


class Op:
    __slots__ = ("eng", "fn", "deps", "idx", "dma", "dma_id")

    def __init__(self, eng, fn, dma):
        self.eng = eng
        self.fn = fn
        self.deps = set()
        self.dma = dma
        self.dma_id = None


class Prog:
    def __init__(self):
        self.ops = {e: [] for e in ENGS}
        self.ndma_k = [0, 0]
        self.signal = {e: set() for e in ENGS}

    def add(self, eng, fn, reads=(), writes=(), dma=False):
        op = Op(eng, fn, dma)
        op.idx = len(self.ops[eng])
        if dma:
            kind = 1 if eng == "pool" else 0
            op.dma_id = 2 * self.ndma_k[kind] + kind
            self.ndma_k[kind] += 1
            me = ("dma", op.dma_id)
            if op.dma_id >= 2 * NDMASEM:
                op.deps.add(("dma", op.dma_id - 2 * NDMASEM))
        else:
            me = (eng, op.idx)
        for r in reads:
            if r.w is not None:
                op.deps.add(r.w)
        for w in writes:
            if w.w is not None:
                op.deps.add(w.w)
            for e, i in w.r.items():
                if e == "dma":
                    for j in i:
                        op.deps.add(("dma", j))
                else:
                    op.deps.add((e, i))
        for r in reads:
            if dma:
                r.r.setdefault("dma", set()).add(op.dma_id)
            else:
                r.r[eng] = op.idx
        for w in writes:
            w.w = me
            w.r = {}
        if eng == "pe" and not dma:
            op.deps = {d for d in op.deps if d[0] != "pe"}
        op.deps.discard(me)
        for d in op.deps:
            if d[0] != "dma":
                self.signal[d[0]].add(d[1])
        self.ops[eng].append(op)
        return op

    def emit(self, nc, final_wait_dmas=()):
        with contextlib.ExitStack() as st:
            sems = {e: st.enter_context(nc.semaphore("s_" + e)) for e in ENGS}
            dsem = [st.enter_context(nc.semaphore("d%d" % i)) for i in range(2 * NDMASEM)]
            block = st.enter_context(nc.Block())
            sigcnt = {}
            for e in ENGS:
                c = 0
                m = {}
                for op in self.ops[e]:
                    if op.idx in self.signal[e]:
                        c += 1
                        m[op.idx] = c
                sigcnt[e] = m
            starters = {"pe": block.tensor, "act": block.scalar, "dve": block.vector,
                        "pool": block.gpsimd, "sp": block.sync}
            for e in ENGS:
                ops = self.ops[e]
                if not ops:
                    continue

                def body(engh, e=e, ops=ops):
                    waited = {}
                    for op in ops:
                        need = {}
                        for d in op.deps:
                            if d[0] == "dma":
                                key = ("d", d[1] % (2 * NDMASEM))
                                val = 16 * (d[1] // (2 * NDMASEM) + 1)
                            else:
                                key = ("e", d[0])
                                val = sigcnt[d[0]][d[1]]
                            if need.get(key, 0) < val:
                                need[key] = val
                        for key, val in need.items():
                            if waited.get(key, 0) >= val:
                                continue
                            waited[key] = val
                            engh.wait_ge(dsem[key[1]] if key[0] == "d" else sems[key[1]], val)
                        ins = op.fn(engh)
                        if op.dma:
                            ins.then_inc(dsem[op.dma_id % (2 * NDMASEM)], 16)
                        elif op.idx in self.signal[e]:
                            ins.then_inc(sems[e], 1)
                    if e == "sp":
                        for j in final_wait_dmas:
                            engh.wait_ge(dsem[j % (2 * NDMASEM)], 16 * (j // (2 * NDMASEM) + 1))

                starters[e](body)


class Buf:
    __slots__ = ("ap", "res")

    def __init__(self, ap, res):
        self.ap = ap
        self.res = res


def build_nc(layers, debug=False):
    NL = len(layers)
    nc = bass.Bass("TRN2", target_bir_lowering=False)
    xin = nc.dram_tensor("xin", [NBLK, TBH, D_MODEL], F32, kind="ExternalInput").ap()
    w_in_d = nc.dram_tensor("w_in_r", [NL, 36, 128, 1024], F32, kind="ExternalInput").ap()
    w_out_d = nc.dram_tensor("w_out_r", [NL, 12, 128, 1024], F32, kind="ExternalInput").ap()
    pool_w_d = nc.dram_tensor("pool_w_r", [NL, 128, 512], F32, kind="ExternalInput").ap()
    par_d = nc.dram_tensor("params", [NL, 128, 4 * NPAR], F32, kind="ExternalInput").ap()
    lnbc_d = nc.dram_tensor("lnbc", [NL, 128, 2048], F32, kind="ExternalInput").ap()
    ident_d = nc.dram_tensor("ident", [128, 128], F32, kind="ExternalInput").ap()
    w4_d = nc.dram_tensor("w4", [NL, 128, 128], F32, kind="ExternalInput").ap()
    dmask_d = nc.dram_tensor("dmask", [128, 32], F32, kind="ExternalInput").ap()
    aux_d = nc.dram_tensor("aux", [128, 2 + 32], F32, kind="ExternalInput").ap()
    out_d = nc.dram_tensor("out", [NBLK, TB, D_MODEL], F32, kind="ExternalOutput").ap()
    if debug:
        dbg_y = nc.dram_tensor("dbg_y", [NBLK * NL, 128, 12 * TBH], BF16, kind="ExternalOutput").ap()
        dbg_x = nc.dram_tensor("dbg_x", [NBLK * NL, 128, 8 * TBH], BF16, kind="ExternalOutput").ap()

    P = Prog()
    NSLOT = 56
    with contextlib.ExitStack() as st:
        def sb(name, shape, dt):
            return st.enter_context(nc.sbuf_tensor(name, shape, dt))

        Rb_t = sb("Rb", [128, 9, 1024], F32)
        xT_t = sb("xT", [128, 8, TBH], BF16)
        yb_t = sb("yb", [128, 12, TBH], BF16)
        wrg_t = sb("wrg", [128, 9, 8, 128], BF16)
        wo_t = sb("wo", [128, 12, 1024], BF16)
        Lm_t = sb("Lm", [128, 4, 8, 32], BF16)
        U_t = sb("Urep", [128, 4, 32 + TBH], BF16)
        w4_t = sb("w4s", [128, 2, 128], F32)
        dm_t = sb("dmask_s", [128, 32], F32)
        pw_t = sb("pw", [128, 2, 4, 128], BF16)
        par_t = sb("par", [128, 2, 4, NPAR], F32)
        lnbc_t = sb("lnbc_s", [128, 2, 1024], F32)
        ids_t = sb("ids", [128, 128], F32)
        idb_t = sb("idb", [128, 128], BF16)
        ones_t = sb("ones", [128, 128], BF16)
        aux_t = sb("aux_s", [128, 34], F32)
        corr_t = sb("corr", [128, 2, 4, 16], F32)
        cnt_t = sb("cnt", [128, 16], F32)
        eps_t = sb("epst", [128, 1], F32)
        bsc_t = sb("bsc", [128, 2, 4], F32)
        tA_t = sb("tailA", [128, NL, 4, 2], F32)
        tB_t = sb("tailB", [128, NL, 4, 30], BF16)
        tC_t = sb("tailC", [128, NL, 4, 15], F32)
        st_t = sb("stt", [128, 2, 2, 6], F32)
        mv_t = sb("mv", [128, 2, 4], F32)
        sd_t = sb("sdt", [128, 2, 2], F32)
        ar_t = sb("arena", [128, NSLOT * 256], F32)
        p2b_t = sb("p2xb", [128, 3, 1024], BF16)
        ps = [st.enter_context(nc.psum_tensor("ps%d" % i, [128, 512], F32)) for i in range(7)]
        pT_t = st.enter_context(nc.psum_tensor("pT", [128, 8, 128], BF16))

        R_Rb = [Res("Rb%d" % i) for i in range(9)]
        R_xT = [Res("xT%d" % i) for i in range(9)]
        R_y = [[Res("y%d_%d" % (c, t)) for t in range(3)] for c in range(12)]
        R_wrg = [Res("wrg%d" % i) for i in range(9)]
        R_wo = [Res("wo%d" % i) for i in range(12)]
        R_dg = Res("Lm")
        R_U = [Res("U0"), Res("U1"), Res("U2")]
        R_w4 = [Res("w4_0"), Res("w4_1")]
        R_dm = Res("dmask")
        R_pw = [Res("pw0"), Res("pw1")]
        R_par = [Res("par0"), Res("par1")]
        R_lnbc = Res("lnbc")
        R_ids, R_idb, R_ones, R_aux, R_corr, R_cnt = (Res("ids"), Res("idb"), Res("ones"), Res("aux"),
                                                      Res("corr"), Res("cnt"))
        R_eps = Res("eps")
        R_bsc = [Res("bsc0"), Res("bsc1")]
        R_tA = [[Res("tA%d_%d" % (l, g)) for g in range(4)] for l in range(NL)]
        R_tB = [[Res("tB%d_%d" % (l, g)) for g in range(4)] for l in range(NL)]
        R_tC = [[Res("tC%d_%d" % (l, g)) for g in range(4)] for l in range(NL)]
        R_st = [Res("st0"), Res("st1")]
        R_mv = [Res("mv0"), Res("mv1")]
        R_sd = [Res("sd0"), Res("sd1")]
        R_slot = [Res("slot%d" % i) for i in range(NSLOT)]
        R_p2 = [Res("p2xn0"), Res("p2xn1")]
        R_p2b = [Res("p2xb0"), Res("p2xb1"), Res("p2xb2")]
        R_ps = [Res("ps%d" % i) for i in range(7)]
        R_pT = Res("pT")

        def arena(s0, ns, dt, n):
            ap = ar_t[:, s0 * 256:(s0 + ns) * 256]
            if dt is BF16:
                ap = ap.bitcast(BF16)
            assert n <= ns * 256 * (2 if dt is BF16 else 1)
            return Buf(ap[:, 0:n], R_slot[s0:s0 + ns])

        bank_ctr = [0]

        def new_bank():
            i = bank_ctr[0] % 7
            bank_ctr[0] += 1
            return Buf(ps[i], [R_ps[i]])

        jobs = []

        def job_state():
            return {"dma": 0, "cast": 0}
        js = job_state()

        NRING = 9

        def pump(upto_job):
            while js["dma"] < len(jobs) and js["dma"] <= upto_job:
                jb = jobs[js["dma"]]
                P.add("pool", lambda e, jb=jb: e.dma_start(out=jb["dst"], in_=jb["src"]), writes=jb["res"], dma=True)
                js["dma"] += 1

        horizon = [-1]

        def pump_tick(n):
            while n > 0 and js["dma"] < len(jobs) and js["dma"] <= horizon[0]:
                jb = jobs[js["dma"]]
                P.add("pool", lambda e, jb=jb: e.dma_start(out=jb["dst"], in_=jb["src"]), writes=jb["res"], dma=True)
                js["dma"] += 1
                n -= 1

        def pump_group(lbi, ci0):
            s0 = lbi * 36 + ci0
            horizon[0] = max(horizon[0], stream_job[min(s0 + NRING - 1, len(stream_job) - 1)])
            need = stream_job[min(s0 + 3, len(stream_job) - 1)]
            while js["dma"] <= need and js["dma"] <= horizon[0]:
                pump_tick(1)
            pump_tick(1)

        win_job = {}
        stream_job = []
        nstream = 0
        LB = [(blk, li) for blk in range(NBLK) for li in range(NL)]
        for lbi, (blk, li) in enumerate(LB):
            jobs.append(dict(src=pool_w_d[li], n=512,
                             dst=pw_t[:, lbi % 2].rearrange("p g d -> p (g d)"), res=[R_pw[lbi % 2]]))
            for ci in range(36):
                slot = nstream % NRING
                nstream += 1
                win_job[(lbi, ci)] = (len(jobs), slot)
                stream_job.append(len(jobs))
                jobs.append(dict(src=w_in_d[li, ci], n=1024,
                                 dst=wrg_t[:, slot].rearrange("p k c -> p (k c)"), res=[R_wrg[slot]]))
                if 12 <= ci < 24:
                    ec = ci - 12
                    jobs.append(dict(src=w_out_d[li, ec], n=1024, dst=wo_t[:, ec, :], res=[R_wo[ec]]))

        P.add("sp", lambda e: e.dma_start(out=ids_t[:], in_=ident_d), writes=[R_ids], dma=True)
        P.add("sp", lambda e: e.dma_start(out=aux_t[:], in_=aux_d), writes=[R_aux], dma=True)
        P.add("sp", lambda e: e.dma_start(out=dm_t[:], in_=dmask_d), writes=[R_dm], dma=True)
        P.add("pool", lambda e: e.memset(U_t[:, :, 0:32], 0.0), writes=[R_U[0]])
        P.add("pool", lambda e: e.tensor_copy(out=idb_t[:], in_=ids_t[:]), reads=[R_ids], writes=[R_idb])
        P.add("pool", lambda e: e.memset(ones_t[:], 1.0), writes=[R_ones])
        P.add("pool", lambda e: e.memset(eps_t[:], LN_EPS), writes=[R_eps])
        for j in range(NBLK):
            for g in range(4):
                w = float(POOL_W[g])
                P.add("dve", lambda e, j=j, w=w: e.tensor_scalar(out=cnt_t[:], in0=aux_t[:, 2 + 16 * j:2 + 16 * (j + 1)],
                                                                  scalar1=1.0, scalar2=w, op0=ALU.add, op1=ALU.min),
                      reads=[R_aux], writes=[R_cnt])
                P.add("dve", lambda e: e.reciprocal(out=cnt_t[:], in_=cnt_t[:]), reads=[R_cnt], writes=[R_cnt])
                P.add("dve", lambda e, j=j, g=g, w=w: e.tensor_scalar(out=corr_t[:, j, g, :], in0=cnt_t[:], scalar1=w,
                                                                      scalar2=None, op0=ALU.mult),
                      reads=[R_cnt], writes=[R_corr])

        out_dmas = []
        p2ctr = [0]

        def transposes_to_xT(src_bf, src_res, tc, blk):
            q0, n = CHUNKS[tc]
            for k in range(8):
                P.add("pe", lambda e, k=k, n=n: e.transpose(out=pT_t[:, k, 0:n], in_=src_bf[0:n, k * 128:(k + 1) * 128],
                                                            identity=idb_t[0:n, 0:n]),
                      reads=src_res + [R_idb], writes=[R_pT] if k in (0, 7) else [])
            if tc == 0:
                P.add("act", lambda e, n=n, q0=q0: e.activation(out=xT_t[:, :, q0:q0 + n], in_=pT_t[:, :, 0:n],
                                                                 func=AF.Identity, scale=aux_t[:, blk:blk + 1]),
                      reads=[R_pT, R_aux], writes=[R_xT[tc]])
            else:
                P.add("act", lambda e, n=n, q0=q0: e.activation(out=xT_t[:, :, q0:q0 + n], in_=pT_t[:, :, 0:n],
                                                                 func=AF.Copy),
                      reads=[R_pT], writes=[R_xT[tc]])

        p0_pp = {}
        p0ctr = [0]

        def p0_load_f32(blk, tc):
            q0, n = CHUNKS[tc]
            P.add("sp", lambda e, tc=tc, q0=q0, n=n, blk=blk: e.dma_start(out=Rb_t[0:n, tc, :], in_=xin[blk, q0:q0 + n, :]),
                  writes=[R_Rb[tc]], dma=True)

        def p0_load_bf16(blk, tc):
            q0, n = CHUNKS[tc]
            pp = p0ctr[0] % 3
            p0ctr[0] += 1
            p0_pp[(blk, tc)] = pp
            P.add("pool", lambda e, q0=q0, n=n, pp=pp, blk=blk: e.dma_start(out=p2b_t[0:n, pp, :], in_=xin[blk, q0:q0 + n, :]),
                  writes=[R_p2b[pp]], dma=True)

        def p0_transpose(blk, tc):
            pp = p0_pp[(blk, tc)]
            transposes_to_xT(p2b_t[:, pp, :], [R_p2b[pp]], tc, blk)

        early_done = set()
        diag0_done = set()
        carry_tr = []

        def emit_par_load(l2):
            li2 = LB[l2][1]
            par2 = par_t[:, l2 % 2]
            P.add("sp", lambda e: e.dma_start(out=par2.rearrange("p g c -> p (g c)"), in_=par_d[li2]),
                  writes=[R_par[l2 % 2]], dma=True)
            P.add("sp", lambda e: e.dma_start(out=w4_t[:, l2 % 2, :], in_=w4_d[li2]), writes=[R_w4[l2 % 2]], dma=True)
            early_done.add(l2)

        def emit_diag(l2, g):
            P.add("dve", lambda e: e.tensor_tensor(
                out=Lm_t[:].rearrange("p s j c -> p (s j) c"),
                in0=dm_t[:].unsqueeze(1).to_broadcast([128, 32, 32]),
                in1=w4_t[:, l2 % 2, 32 * g:32 * g + 32].unsqueeze(2).to_broadcast([128, 32, 32]), op=ALU.mult),
                reads=[R_dm, R_w4[l2 % 2]], writes=[R_dg])

        for lbi, (blk, li) in enumerate(LB):
            last_layer = (li == NL - 1)
            skip0 = (blk >= 1)
            noy0 = last_layer or skip0
            save_tails = (blk + 1 < NBLK)
            par = par_t[:, lbi % 2]
            Rpar = R_par[lbi % 2]
            pw = pw_t[:, lbi % 2]
            Rpw = R_pw[lbi % 2]

            def pcol(g, c, par=par):
                return par[:, g, c:c + 1]

            if lbi not in early_done:
                emit_par_load(lbi)
            P.add("sp", lambda e, li=li: e.dma_start(out=lnbc_t[:].rearrange("p a d -> p (a d)"), in_=lnbc_d[li]),
                  writes=[R_lnbc], dma=True)

            if li == 0 and blk == 0:
                for tc in range(9):
                    p0_load_f32(blk, tc)
                for tc in range(3):
                    p0_load_bf16(blk, tc)
                for tc in range(9):
                    p0_transpose(blk, tc)
                    if tc + 3 < 9:
                        p0_load_bf16(blk, tc + 3)

            if debug:
                o = P.add("sp", lambda e, lbi=lbi: e.dma_start(out=dbg_x[lbi], in_=xT_t[:].rearrange("p c t -> p (c t)")),
                          reads=R_xT, dma=True)
                out_dmas.append(o.dma_id)
            sz = arena(0, 9, BF16, 4 * TBH)
            cb = arena(9, 9, BF16, 4 * TBH)
            ubs = [arena(18, 3, BF16, 32 + TBH), arena(21, 3, BF16, 32 + TBH)]

            def B_DIAG(g):
                if g == 0 and lbi in diag0_done:
                    return
                emit_diag(lbi, g)

            def B_W(g, t):
                off, N = TILES[t]
                ub = ubs[g % 2]
                if skip0 and t == 0:
                    P.add("pool", lambda e, ub=ub: e.memset(ub.ap[:, 0:96], 0.0), writes=ub.res)
                    P.add("pool", lambda e, ub=ub, g=g, li=li: e.tensor_copy(out=ub.ap[:, 66:96], in_=tB_t[:, li, g, :]),
                          reads=[R_tB[li][g]], writes=ub.res)
                    pump_group(lbi, 3 * g)
                    return
                if t == 0:
                    P.add("pool", lambda e, ub=ub: e.memset(ub.ap[:, 0:32], 0.0), writes=ub.res)
                    pump_group(lbi, 3 * g)
                xres = [R_xT[c] for c in TILE_CHUNKS[t]]
                banks = []
                for c3 in range(3):
                    ci = 3 * g + c3
                    jidx, slot = win_job[(lbi, ci)]
                    if noy0 and t == 0 and c3 == 2:
                        banks.append(None)
                        continue
                    bk = new_bank()
                    banks.append(bk)
                    for k in range(8):
                        P.add("pe", lambda e, bk=bk, slot=slot, k=k, off=off, N=N: e.matmul(
                            bk.ap[:, 0:N], lhsT=wrg_t[:, slot, k, :], rhs=xT_t[:, k, off:off + N],
                            start=(k == 0), stop=(k == 7)),
                            reads=[R_wrg[slot]] + xres, writes=bk.res if k in (0, 7) else [])
                Pv, Pg, Pz = banks
                par2 = (g * 3 + t) % 2
                sg = arena(24, 2, F32, 512)
                P.add("act", lambda e, sg=sg, Pg=Pg, N=N: e.activation(out=sg.ap[:, 0:N], in_=Pg.ap[:, 0:N], func=AF.Tanh,
                                                                       scale=0.5),
                      reads=Pg.res, writes=sg.res)
                P.add("dve", lambda e, ub=ub, Pv=Pv, sg=sg, off=off, N=N: e.scalar_tensor_tensor(
                    out=ub.ap[:, 32 + off:32 + off + N], in0=sg.ap[:, 0:N], scalar=1.0, in1=Pv.ap[:, 0:N],
                    op0=ALU.add, op1=ALU.mult),
                    reads=Pv.res + sg.res, writes=ub.res)
                if save_tails and t == 2:
                    P.add("pool", lambda e, ub=ub, g=g, li=li: e.tensor_copy(out=tB_t[:, li, g, :], in_=ub.ap[:, 1090:1120]),
                          reads=ub.res, writes=[R_tB[li][g]])
                if noy0 and t == 0:
                    return
                P.add("act", lambda e, sz=sz, Pz=Pz, off=off, N=N, g=g: e.activation(
                    out=sz.ap[:, g * TBH + off:g * TBH + off + N], in_=Pz.ap[:, 0:N], func=AF.Silu),
                    reads=Pz.res, writes=sz.res)

            def B_COMBINE():
                pass

            def B_REP(g, t):
                off, N = TILES[t]
                ub = ubs[g % 2]
                for s4 in range(4):
                    bk = new_bank()
                    for i4 in range(4):
                        P.add("pe", lambda e, bk=bk, ub=ub, s4=s4, i4=i4, off=off, N=N: e.matmul(
                            bk.ap[32 * i4:32 * i4 + 32, 0:N], lhsT=idb_t[:, 32 * s4:32 * s4 + 32],
                            rhs=ub.ap[:, 29 + off + i4:29 + off + i4 + N], start=True, stop=True, tile_position=(0, 32 * i4)),
                            reads=[R_idb] + ub.res, writes=bk.res if i4 in (0, 3) else [])
                    P.add("act", lambda e, bk=bk, s4=s4, off=off, N=N: e.activation(
                        out=U_t[:, s4, 29 + off:29 + off + N], in_=bk.ap[:, 0:N], func=AF.Copy),
                        reads=bk.res, writes=[R_U[t]])

            def B_CONV(g, t):
                off, N = TILES[t]
                if noy0 and t == 0:
                    return
                Pc = new_bank()
                ures = [R_U[t]] + ([R_U[t - 1]] if t > 0 else [])
                for j in range(8):
                    K = 128
                    for s4 in range(4):
                        first = (j == 0 and s4 == 0)
                        last = (j == 7 and s4 == 3)
                        P.add("pe", lambda e, Pc=Pc, j=j, s4=s4, K=K, off=off, N=N: e.matmul(
                            Pc.ap[32 * s4:32 * s4 + 32, 0:N], lhsT=Lm_t[0:K, s4, j, :],
                            rhs=U_t[0:K, s4, off + 1 + 4 * j:off + 1 + 4 * j + N],
                            start=(j == 0), stop=(j == 7), tile_position=(0, 32 * s4)),
                            reads=[R_dg] + ures, writes=Pc.res if (first or last) else [])
                P.add("act", lambda e, cb=cb, Pc=Pc, off=off, N=N, g=g, pc=pcol: e.activation(
                    out=cb.ap[:, g * TBH + off:g * TBH + off + N], in_=Pc.ap[:, 0:N], func=AF.Identity,
                    scale=0.5, bias=pc(g, 4)), reads=Pc.res + [Rpar], writes=cb.res)

            def B_FINAL_stages(t, bs):
                off, N = TILES[t]
                m = arena(28 + 6 * bs, 2, F32, 512)
                X = arena(30 + 6 * bs, 2, F32, 512)
                rstd = arena(32 + 6 * bs, 2, F32, 512)
                bank = {}

                def s_stats():
                    Psum = bank["sum"] = new_bank()
                    Psq = bank["sq"] = new_bank()
                    for g in range(4):
                        P.add("pe", lambda e, g=g: e.matmul(
                            Psum.ap[:, 0:N], lhsT=ones_t[:], rhs=cb.ap[:, g * TBH + off:g * TBH + off + N],
                            start=(g == 0), stop=(g == 3)), reads=[R_ones] + cb.res, writes=Psum.res if g in (0, 3) else [])
                    for g in range(4):
                        sqt = arena(26 + (g % 2), 1, BF16, 512)
                        P.add("act", lambda e, sqt=sqt, g=g: e.activation(
                            out=sqt.ap[:, 0:N], in_=cb.ap[:, g * TBH + off:g * TBH + off + N], func=AF.Square),
                            reads=cb.res, writes=sqt.res)
                        P.add("pe", lambda e, sqt=sqt, g=g: e.matmul(
                            Psq.ap[:, 0:N], lhsT=ones_t[:], rhs=sqt.ap[:, 0:N], start=(g == 0), stop=(g == 3)),
                            reads=[R_ones] + sqt.res, writes=Psq.res if g in (0, 3) else [])

                def s_var():
                    Psum, Psq = bank["sum"], bank["sq"]
                    P.add("dve", lambda e: e.tensor_scalar(out=m.ap[:, 0:N], in0=Psum.ap[:, 0:N],
                                                            scalar1=1.0 / 512, scalar2=None, op0=ALU.mult),
                          reads=Psum.res, writes=m.res)
                    P.add("dve", lambda e: e.tensor_tensor(out=X.ap[:, 0:N], in0=m.ap[:, 0:N], in1=m.ap[:, 0:N],
                                                            op=ALU.mult), reads=m.res, writes=X.res)
                    P.add("dve", lambda e: e.scalar_tensor_tensor(
                        out=X.ap[:, 0:N], in0=Psq.ap[:, 0:N], scalar=1.0 / 512, in1=X.ap[:, 0:N], op0=ALU.mult,
                        op1=ALU.subtract), reads=Psq.res + X.res, writes=X.res)

                def s_rstd():
                    P.add("act", lambda e: e.activation(out=X.ap[:, 0:N], in_=X.ap[:, 0:N], func=AF.Sqrt,
                                                        bias=eps_t[:, 0:1]),
                          reads=X.res + [R_eps], writes=X.res)
                    P.add("dve", lambda e: e.reciprocal(out=rstd.ap[:, 0:N], in_=X.ap[:, 0:N]),
                          reads=X.res, writes=rstd.res)

                def s_norm(g):
                    P.add("dve" if bs == 0 else "pool", lambda e, g=g: e.tensor_tensor(
                        out=X.ap[:, 0:N], in0=cb.ap[:, g * TBH + off:g * TBH + off + N], in1=m.ap[:, 0:N], op=ALU.subtract),
                        reads=cb.res + m.res, writes=X.res)
                    P.add("dve", lambda e: e.tensor_tensor(
                        out=X.ap[:, 0:N], in0=X.ap[:, 0:N], in1=rstd.ap[:, 0:N], op=ALU.mult),
                        reads=X.res + rstd.res, writes=X.res)

                def s_gate(g):
                    P.add("act", lambda e, g=g, pc=pcol: e.activation(
                        out=X.ap[:, 0:N], in_=X.ap[:, 0:N], func=AF.Silu, scale=pc(g, 5), bias=pc(g, 6)),
                        reads=X.res + [Rpar], writes=X.res)
                    P.add("pool", lambda e, g=g: e.tensor_tensor(
                        out=yb_t[:, 4 + g, off:off + N], in0=X.ap[:, 0:N], in1=sz.ap[:, g * TBH + off:g * TBH + off + N],
                        op=ALU.mult), reads=X.res + sz.res, writes=[R_y[4 + g][t]])

                st = [lambda: (s_stats(), s_var()), s_rstd, lambda: s_norm(0)]
                for g in range(4):
                    st.append(lambda g=g: (s_gate(g), s_norm(g + 1) if g < 3 else None))
                return st

            fin_wait = []
            fin_active = {}

            def fin_tick():
                for bs in (0, 1):
                    if bs not in fin_active and fin_wait:
                        fin_active[bs] = B_FINAL_stages(fin_wait.pop(0), bs)
                for bs in (0, 1):
                    st = fin_active.get(bs)
                    if st:
                        st.pop(0)()
                    if bs in fin_active and not fin_active[bs]:
                        del fin_active[bs]

            def B_FINAL(t):
                if noy0 and t == 0:
                    return
                fin_wait.append(t)

            B_DIAG(0)
            steps = [(g, t) for g in range(4) for t in range(3)]

            def after_conv(gt):
                if gt[0] == 3:
                    B_FINAL(gt[1])
                if gt[1] == 2 and gt[0] < 3:
                    B_DIAG(gt[0] + 1)

            for n, (g, t) in enumerate(steps):
                if g == 0 and t >= 1 and carry_tr:
                    for _ in range(1 if t == 1 and len(carry_tr) > 1 else len(carry_tr)):
                        ptc, ppp, pblk = carry_tr.pop(0)
                        transposes_to_xT(p2b_t[:, ppp, :], [R_p2b[ppp]], ptc, pblk)
                B_W(g, t)
                pump_tick(2)
                fin_tick()
                if n >= 1:
                    B_REP(*steps[n - 1])
                if n >= 2:
                    B_CONV(*steps[n - 2])
                    after_conv(steps[n - 2])
            B_REP(*steps[-1])
            B_CONV(*steps[-2])
            after_conv(steps[-2])
            fin_tick()
            B_CONV(*steps[-1])
            after_conv(steps[-1])

            def A_PREP(g):
                ua = arena(40 + 5 * (g % 2), 5, F32, 2 + TBH)
                if skip0:
                    P.add("pool", lambda e, ua=ua: e.memset(ua.ap[:, 0:66], 0.0), writes=ua.res)
                    P.add("pool", lambda e, ua=ua, g=g, li=li: e.tensor_copy(out=ua.ap[:, 64:66], in_=tA_t[:, li, g, :]),
                          reads=[R_tA[li][g]], writes=ua.res)
                else:
                    P.add("pool", lambda e, ua=ua: e.memset(ua.ap[:, 0:2], 0.0), writes=ua.res)

            A_PREP(0)
            for g in range(4):
                gp = g % 2
                ua = arena(40 + 5 * gp, 5, F32, 2 + TBH)
                if g + 1 < 4:
                    A_PREP(g + 1)
                pump_group(lbi, 12 + 4 * g)
                for t, (off, N) in enumerate(TILES):
                    if skip0 and t == 0:
                        B_COMBINE()
                        fin_tick()
                        continue
                    xres = [R_xT[c] for c in TILE_CHUNKS[t]]
                    banks = []
                    for c4 in range(4):
                        ci = 12 + 4 * g + c4
                        jidx, slot = win_job[(lbi, ci)]
                        if noy0 and t == 0 and c4 in (0, 3):
                            banks.append(None)
                            continue
                        bk = new_bank()
                        banks.append(bk)
                        for k in range(8):
                            P.add("pe", lambda e, bk=bk, slot=slot, k=k, off=off, N=N: e.matmul(
                                bk.ap[:, 0:N], lhsT=wrg_t[:, slot, k, :], rhs=xT_t[:, k, off:off + N],
                                start=(k == 0), stop=(k == 7)),
                                reads=[R_wrg[slot]] + xres, writes=bk.res if k in (0, 7) else [])
                    Pbg, Pcg, Pv, Pz = banks
                    par2 = (g * 3 + t) % 2
                    t1 = arena((50, 20)[par2], 2, F32, 512)
                    cA = arena((52, 22)[par2], 2, F32, 512)
                    sA = arena((54, 24)[par2], 2, F32, 512)
                    y1 = arena(18, 2, F32, 512)
                    P.add("act", lambda e, t1=t1, Pv=Pv, N=N: e.activation(out=t1.ap[:, 0:N], in_=Pv.ap[:, 0:N], func=AF.Copy),
                          reads=Pv.res, writes=t1.res)
                    P.add("dve", lambda e, ua=ua, Pcg=Pcg, t1=t1, off=off, N=N: e.tensor_tensor(
                        out=ua.ap[:, 2 + off:2 + off + N], in0=Pcg.ap[:, 0:N], in1=t1.ap[:, 0:N], op=ALU.mult),
                        reads=Pcg.res + t1.res, writes=ua.res)
                    if save_tails and t == 2:
                        P.add("pool", lambda e, ua=ua, g=g, li=li: e.tensor_copy(out=tA_t[:, li, g, :], in_=ua.ap[:, 1088:1090]),
                              reads=ua.res, writes=[R_tA[li][g]])
                    if not (noy0 and t == 0):
                        P.add("act", lambda e, sA=sA, Pz=Pz, N=N: e.activation(out=sA.ap[:, 0:N], in_=Pz.ap[:, 0:N], func=AF.Silu),
                              reads=Pz.res, writes=sA.res)
                        P.add("dve", lambda e, Pbg=Pbg, sA=sA, N=N: e.tensor_tensor(
                            out=sA.ap[:, 0:N], in0=Pbg.ap[:, 0:N], in1=sA.ap[:, 0:N], op=ALU.mult),
                            reads=Pbg.res + sA.res, writes=sA.res)
                    if not (noy0 and t == 0):
                        P.add("act", lambda e, cA=cA, ua=ua, off=off, N=N, g=g, pc=pcol: e.activation(
                            out=cA.ap[:, 0:N], in_=ua.ap[:, off:off + N], func=AF.Identity, scale=pc(g, 0), bias=pc(g, 3)),
                            reads=ua.res + [Rpar], writes=cA.res)
                        for k in (1, 2):
                            P.add("dve", lambda e, cA=cA, ua=ua, off=off, N=N, g=g, k=k, pc=pcol: e.scalar_tensor_tensor(
                                out=cA.ap[:, 0:N], in0=ua.ap[:, off + k:off + k + N], scalar=pc(g, k), in1=cA.ap[:, 0:N],
                                op0=ALU.mult, op1=ALU.add), reads=ua.res + [Rpar] + cA.res, writes=cA.res)
                        P.add("dve", lambda e, sA=sA, cA=cA, off=off, N=N, g=g: e.tensor_tensor(
                            out=yb_t[:, g, off:off + N], in0=sA.ap[:, 0:N], in1=cA.ap[:, 0:N], op=ALU.mult),
                            reads=sA.res + cA.res, writes=[R_y[g][t]])
                    B_COMBINE()
                    pump_tick(3)
                    fin_tick()

            pump_group(lbi, 28)
            while fin_active or fin_wait:
                fin_tick()
            cus = [arena(0, 5, F32, 16 + TBH), arena(5, 5, F32, 16 + TBH)]
            sCs = [arena(31, 3, BF16, TBH), arena(34, 3, BF16, TBH)]
            lv = [arena(10, 5, F32, 16 + TBH), arena(26, 5, F32, 16 + TBH)]
            dpls = [arena(37, 3, BF16, TBH), arena(37, 3, BF16, TBH)]

            def C_W(g):
                cu, sC = cus[CG.index(g) % 2], sCs[CG.index(g) % 2]
                if skip0:
                    if CG.index(g) < 2:
                        P.add("pool", lambda e, cu=cu: e.memset(cu.ap[:, 0:80], 0.0), writes=cu.res)
                    P.add("pool", lambda e, cu=cu, g=g, li=li: e.tensor_copy(out=cu.ap[:, 65:80], in_=tC_t[:, li, g, :]),
                          reads=[R_tC[li][g]], writes=cu.res)
                elif CG.index(g) < 2:
                    P.add("pool", lambda e, cu=cu: e.memset(cu.ap[:, 0:16], 0.0), writes=cu.res)
                pump_group(lbi, 28 + 2 * CG.index(g))
                for t, (off, N) in enumerate(TILES):
                    if skip0 and t == 0:
                        continue
                    xres = [R_xT[c] for c in TILE_CHUNKS[t]]
                    banks = []
                    for c2 in range(2):
                        ci = 28 + 2 * CG.index(g) + c2
                        jidx, slot = win_job[(lbi, ci)]
                        if noy0 and t == 0 and c2 == 1:
                            banks.append(None)
                            continue
                        bk = new_bank()
                        banks.append(bk)
                        for k in range(8):
                            P.add("pe", lambda e, bk=bk, slot=slot, k=k, off=off, N=N: e.matmul(
                                bk.ap[:, 0:N], lhsT=wrg_t[:, slot, k, :], rhs=xT_t[:, k, off:off + N],
                                start=(k == 0), stop=(k == 7)),
                                reads=[R_wrg[slot]] + xres, writes=bk.res if k in (0, 7) else [])
                    Pu, Pz = banks
                    P.add("act", lambda e, cu=cu, Pu=Pu, off=off, N=N: e.activation(
                        out=cu.ap[:, 16 + off:16 + off + N], in_=Pu.ap[:, 0:N], func=AF.Copy), reads=Pu.res, writes=cu.res)
                    if save_tails and t == 2:
                        P.add("pool", lambda e, cu=cu, g=g, li=li: e.tensor_copy(out=tC_t[:, li, g, :], in_=cu.ap[:, 1089:1104]),
                              reads=cu.res, writes=[R_tC[li][g]])
                    if noy0 and t == 0:
                        continue
                    P.add("act", lambda e, sC=sC, Pz=Pz, off=off, N=N: e.activation(
                        out=sC.ap[:, off:off + N], in_=Pz.ap[:, 0:N], func=AF.Silu), reads=Pz.res, writes=sC.res)

            def C_CHAIN(g, i):
                cu, dpl = cus[i % 2], dpls[i % 2]
                if i == 0:
                    for l2 in lv:
                        P.add("pool", lambda e, l2=l2: e.memset(l2.ap[:, 0:16], 0.0), writes=l2.res)
                src = cu
                sh = 1
                for lev in range(g + 1):
                    dst = lv[lev % 2]
                    P.add("dve", lambda e, src=src, dst=dst, sh=sh: e.tensor_tensor(
                        out=dst.ap[:, 16:16 + TBH], in0=src.ap[:, 16:16 + TBH], in1=src.ap[:, 16 - sh:16 - sh + TBH], op=ALU.add),
                        reads=src.res, writes=dst.res)
                    src = dst
                    sh *= 2
                sw = src
                P.add("dve", lambda e, sw=sw, g=g, blk=blk: e.tensor_tensor(
                    out=sw.ap[:, 16 + HALO:16 + HALO + 16], in0=sw.ap[:, 16 + HALO:16 + HALO + 16], in1=corr_t[:, blk, g, :], op=ALU.mult),
                    reads=sw.res + [R_corr], writes=sw.res)
                P.add("dve", lambda e, sw=sw, cu=cu, dpl=dpl, g=g: e.scalar_tensor_tensor(
                    out=dpl.ap[:, 0:TBH], in0=sw.ap[:, 16:16 + TBH], scalar=1.0 / POOL_W[g], in1=cu.ap[:, 16:16 + TBH],
                    op0=ALU.mult, op1=ALU.subtract), reads=sw.res + cu.res, writes=dpl.res)

            P.add("dve", lambda e, par=par, lbi=lbi: e.tensor_tensor(out=bsc_t[:, lbi % 2, :], in0=par[:, :, 7], in1=par[:, :, 8],
                                                                     op=ALU.mult), reads=[Rpar], writes=[R_bsc[lbi % 2]])

            def C_POOL(g):
                sC, dpl = sCs[CG.index(g) % 2], dpls[CG.index(g) % 2]
                for t, (off, N) in enumerate(TILES):
                    if noy0 and t == 0:
                        continue
                    Pp = new_bank()
                    P.add("pe", lambda e, Pp=Pp, dpl=dpl, g=g, off=off, N=N, pw=pw: e.matmul(
                        Pp.ap[:, 0:N], lhsT=pw[:, g, :], rhs=dpl.ap[:, off:off + N], start=True, stop=True),
                        reads=[Rpw] + dpl.res, writes=Pp.res)
                    q = arena(15, 2, F32, 512)
                    P.add("act", lambda e, q=q, Pp=Pp, g=g, N=N, pc=pcol, lbi=lbi: e.activation(
                        out=q.ap[:, 0:N], in_=Pp.ap[:, 0:N], func=AF.Identity, scale=pc(g, 8), bias=bsc_t[:, lbi % 2, g:g + 1]),
                        reads=Pp.res + [Rpar, R_bsc[lbi % 2]], writes=q.res)
                    P.add("pool", lambda e, q=q, sC=sC, g=g, off=off, N=N: e.tensor_tensor(
                        out=yb_t[:, 8 + g, off:off + N], in0=q.ap[:, 0:N], in1=sC.ap[:, off:off + N], op=ALU.mult),
                        reads=q.res + sC.res, writes=[R_y[8 + g][t]])

            EC_ORDER = list(range(8)) + [8 + CG[0], 8 + CG[1], 8 + CG[2], 8 + CG[3]]
            p2_banks = {}

            def p2_mm(tc, lo, hi):
                q0, n = CHUNKS[tc]
                t = 0 if tc == 0 else (1 if tc <= 4 else 2)
                if tc not in p2_banks:
                    p2_banks[tc] = [new_bank(), new_bank()]
                Po = p2_banks[tc]
                for h in range(2):
                    for idx in range(lo, hi):
                        ec = EC_ORDER[idx]
                        P.add("pe", lambda e, Po=Po, h=h, ec=ec, idx=idx, q0=q0, n=n: e.matmul(
                            Po[h].ap[0:n, :], lhsT=yb_t[:, ec, q0:q0 + n], rhs=wo_t[:, ec, h * 512:(h + 1) * 512],
                            start=(idx == 0), stop=(idx == 11)),
                            reads=[R_y[ec][t], R_wo[ec]], writes=Po[h].res if idx in (0, 11) else [])
                return Po

            p2_first = [1, 2] if noy0 else [0, 1]
            C_W(CG[0])
            for i, g in enumerate(CG):
                if i + 1 < 4:
                    C_W(CG[i + 1])
                C_CHAIN(g, i)
                if i == 3:
                    for tc in p2_first:
                        p2_mm(tc, 0, 10)
                C_POOL(g)

            if lbi + 1 < len(LB):
                pump_group(lbi + 1, 0)
            if debug:
                o = P.add("sp", lambda e, lbi=lbi: e.dma_start(out=dbg_y[lbi], in_=yb_t[:].rearrange("p c t -> p (c t)")),
                          reads=[r for rr in R_y for r in rr], dma=True)
                out_dmas.append(o.dma_id)
            tr_pending = []
            back_pending = []
            nx_todo = list(range(1, 9)) if (last_layer and blk + 1 < NBLK) else []
            nx_loaded = []
            for tc in range(9):
                if noy0 and tc == 0:
                    continue
                q0, n = CHUNKS[tc]
                t = 0 if tc == 0 else (1 if tc <= 4 else 2)
                Po = p2_mm(tc, 10, 12) if tc in p2_banks else p2_mm(tc, 0, 12)
                if lbi + 1 < len(LB):
                    if tc == 3:
                        emit_par_load(lbi + 1)
                    if tc == 6:
                        emit_diag(lbi + 1, 0)
                        diag0_done.add(lbi + 1)
                if len(tr_pending) == 2:
                    ptc, ppp = tr_pending.pop(0)
                    transposes_to_xT(p2b_t[:, ppp, :], [R_p2b[ppp]], ptc, blk)
                if last_layer and blk + 1 < NBLK:
                    if nx_loaded:
                        p0_transpose(blk + 1, nx_loaded.pop(0))
                    for _ in range(2 if tc == 1 else 1):
                        if nx_todo:
                            c = nx_todo.pop(0)
                            p0_load_bf16(blk + 1, c)
                            nx_loaded.append(c)
                    if tc == 1 and nx_loaded:
                        pass
                pp = p2ctr[0] % 2
                p2ctr[0] += 1
                Rt = Rb_t[0:n, tc, :]
                for h in range(2):
                    P.add("dve", lambda e, Rt=Rt, Po=Po, h=h, n=n: e.scalar_tensor_tensor(
                        out=Rt[:, h * 512:(h + 1) * 512], in0=Rt[:, h * 512:(h + 1) * 512], scalar=ALPHA,
                        in1=Po[h].ap[0:n, :], op0=ALU.mult, op1=ALU.add), reads=[R_Rb[tc]] + Po[h].res, writes=[R_Rb[tc]])
                for h in range(2):
                    P.add("dve", lambda e, Rt=Rt, h=h, n=n, pp=pp: e.bn_stats(out=st_t[0:n, pp, h, :],
                                                                               in_=Rt[:, h * 512:(h + 1) * 512]),
                          reads=[R_Rb[tc]], writes=[R_st[pp]])
                P.add("dve", lambda e, n=n, pp=pp: e.bn_aggr(out=mv_t[0:n, pp, 0:2],
                                                             in_=st_t[0:n, pp].rearrange("p a s -> p (a s)")),
                      reads=[R_st[pp]], writes=[R_mv[pp]])
                P.add("act", lambda e, n=n, pp=pp: e.activation(out=sd_t[0:n, pp, 1:2], in_=mv_t[0:n, pp, 1:2], func=AF.Sqrt,
                                                                bias=eps_t[0:n, 0:1]),
                      reads=[R_mv[pp], R_eps], writes=[R_sd[pp]])
                P.add("dve", lambda e, n=n, pp=pp: e.reciprocal(out=mv_t[0:n, pp, 2:3], in_=sd_t[0:n, pp, 1:2]),
                      reads=[R_sd[pp], R_mv[pp]], writes=[R_mv[pp]])
                P.add("dve", lambda e, n=n, pp=pp: e.tensor_scalar(out=mv_t[0:n, pp, 3:4], in0=mv_t[0:n, pp, 0:1],
                                                                   scalar1=mv_t[0:n, pp, 2:3], scalar2=-1.0, op0=ALU.mult, op1=ALU.mult),
                      reads=[R_mv[pp]], writes=[R_mv[pp]])
                xnb = arena(4 * pp, 4, F32, 1024)
                xn = xnb.ap[0:n, :]
                P.add("act", lambda e, xn=xn, Rt=Rt, n=n, pp=pp: e.activation(out=xn, in_=Rt, func=AF.Identity,
                                                                               scale=mv_t[0:n, pp, 2:3], bias=mv_t[0:n, pp, 3:4]),
                      reads=[R_Rb[tc], R_mv[pp]], writes=xnb.res)
                def back(xn=xn, xnb=xnb, Rt=Rt, n=n, q0=q0, tc=tc, pp=pp, blk=blk, last_layer=last_layer):
                    P.add("pool", lambda e: e.tensor_tensor(out=xn, in0=xn, in1=lnbc_t[0:n, 0, :], op=ALU.mult),
                          reads=xnb.res + [R_lnbc], writes=xnb.res)
                    P.add("pool", lambda e: e.tensor_tensor(out=Rt, in0=xn, in1=lnbc_t[0:n, 1, :], op=ALU.add),
                          reads=xnb.res + [R_lnbc], writes=[R_Rb[tc]])
                    if last_layer:
                        o = P.add("sp", lambda e: e.dma_start(out=out_d[blk, q0 - HALO:q0 - HALO + n, :], in_=Rt),
                                  reads=[R_Rb[tc]], dma=True)
                        out_dmas.append(o.dma_id)
                    else:
                        P.add("act", lambda e: e.activation(out=p2b_t[0:n, pp, :], in_=Rt, func=AF.Copy),
                              reads=[R_Rb[tc]], writes=[R_p2b[pp]])
                        tr_pending.append((tc, pp))
                while back_pending:
                    back_pending.pop(0)()
                back_pending.append(back)
            if len(tr_pending) == 2:
                ptc, ppp = tr_pending.pop(0)
                transposes_to_xT(p2b_t[:, ppp, :], [R_p2b[ppp]], ptc, blk)
            while back_pending:
                back_pending.pop(0)()
            for ptc, ppp in tr_pending:
                if CHUNK_TILE[ptc] == 2:
                    carry_tr.append((ptc, ppp, blk))
                else:
                    transposes_to_xT(p2b_t[:, ppp, :], [R_p2b[ppp]], ptc, blk)
            if last_layer and blk + 1 < NBLK:
                while nx_loaded or nx_todo:
                    if nx_loaded:
                        c = nx_loaded.pop(0)
                        if CHUNK_TILE[c] == 2 and not nx_todo:
                            carry_tr.append((c, p0_pp[(blk + 1, c)], blk + 1))
                        else:
                            p0_transpose(blk + 1, c)
                    if nx_todo:
                        c = nx_todo.pop(0)
                        p0_load_bf16(blk + 1, c)
                        nx_loaded.append(c)
                for c in range(1, 9):
                    p0_load_f32(blk + 1, c)

        P.emit(nc, final_wait_dmas=out_dmas)
    return nc


def _prep_weights(layers, w_in, conv_a_w, conv_a_b, conv_b_w, conv_b_b, ln_b_g, ln_b_b,
                  pool_w, pool_b, pool_scale, w_out, ln_g, ln_b):
    NL = len(layers)
    w_in_r = np.empty((NL, 36, 128, 1024), np.float32)
    w_out_r = np.empty((NL, 12, 128, 1024), np.float32)
    pool_w_r = np.empty((NL, 128, 512), np.float32)
    params = np.empty((NL, 128, 4, NPAR), np.float32)
    lnbc = np.empty((NL, 128, 2048), np.float32)
    for i, l in enumerate(layers):
        wi = np.asarray(w_in[l], np.float32).reshape(8, 128, D_IN)
        for ci, base in enumerate(COLS):
            w_in_r[i, ci] = wi[:, :, base:base + 128].transpose(1, 0, 2).reshape(128, 1024)
        w_out_r[i] = np.asarray(w_out[l], np.float32).reshape(12, 128, 1024)
        pool_w_r[i] = np.asarray(pool_w[l], np.float32).transpose(1, 0, 2).reshape(128, 512)
        caw = np.asarray(conv_a_w[l], np.float32).reshape(3, 4, 128)
        cbw = np.asarray(conv_b_w[l], np.float32).reshape(31, 4, 128)
        params[i, :, :, 0:3] = caw.transpose(2, 1, 0)
        params[i, :, :, 3] = np.asarray(conv_a_b[l], np.float32).reshape(4, 128).T
        params[i, :, :, 4] = np.asarray(conv_b_b[l], np.float32).reshape(4, 128).T
        params[i, :, :, 5] = np.asarray(ln_b_g[l], np.float32).reshape(4, 128).T
        params[i, :, :, 6] = np.asarray(ln_b_b[l], np.float32).reshape(4, 128).T
        params[i, :, :, 7] = np.asarray(pool_b[l], np.float32).reshape(4, 128).T
        params[i, :, :, 8] = np.asarray(pool_scale[l], np.float32).reshape(4, 128).T
        lnbc[i, :, 0:1024] = np.asarray(ln_g[l], np.float32)[None, :]
        lnbc[i, :, 1024:2048] = np.asarray(ln_b[l], np.float32)[None, :]
    w4 = np.zeros((NL, 128, 4, 4, 8), np.float32)
    for i, l in enumerate(layers):
        cbw = np.zeros((32, 512), np.float32)
        cbw[1:32] = np.asarray(conv_b_w[l], np.float32)
        w4[i] = cbw.reshape(8, 4, 4, 4, 32).transpose(1, 4, 2, 3, 0).reshape(128, 4, 4, 8)
    dmask = np.tile(np.eye(32, dtype=np.float32), (4, 1))
    return dict(w4=w4.reshape(NL, 128, 128), dmask=dmask, w_in_r=w_in_r, w_out_r=w_out_r, pool_w_r=pool_w_r,
                params=params.reshape(NL, 128, 4 * NPAR), lnbc=lnbc,
                ident=np.eye(128, dtype=np.float32))


def _shard_x(x):
    shards = []
    for c in range(NCORES):
        b = c // (NCORES // BATCH)
        s0 = (c % (NCORES // BATCH)) * TPC
        xin = np.zeros((NBLK, TBH, D_MODEL), np.float32)
        aux = np.zeros((128, 2 + 32), np.float32)
        for j in range(NBLK):
            start = s0 + j * TB
            lo = max(0, start - HALO)
            xin[j, HALO - (start - lo):] = x[b, lo:start + TB]
            aux[:, j] = 0.0 if start == 0 else 1.0
            aux[:, 2 + 16 * j:2 + 16 * (j + 1)] = (start + np.arange(16, dtype=np.float32))[None, :]
        shards.append((xin, aux))
    return shards


def _gather(res):
    out = np.empty((BATCH, SEQ, D_MODEL), np.float32)
    for c in range(NCORES):
        b = c // (NCORES // BATCH)
        s0 = (c % (NCORES // BATCH)) * TPC
        out[b, s0:s0 + TPC] = np.asarray(res[c]["out"], np.float32).reshape(TPC, D_MODEL)
    return out


_NC_CACHE = {}


def _run(layers, x, weights):
    key = tuple(layers)
    if key not in _NC_CACHE:
        _NC_CACHE[key] = build_nc(list(layers))
    nc = _NC_CACHE[key]
    wts = _prep_weights(list(layers), **weights)
    shards = _shard_x(np.asarray(x, np.float32))
    in_maps = []
    for c in range(NCORES):
        m = dict(wts)
        m["xin"], m["aux"] = shards[c]
        in_maps.append(m)
    res = run_bass_kernel_spmd(nc, in_maps, core_ids=list(range(NCORES)))
    return _gather(res.results)


FUSED = True


def kernel(x, w_in, conv_a_w, conv_a_b, conv_b_w, conv_b_b, ln_b_g, ln_b_b,
           pool_w, pool_b, pool_scale, w_out, ln_g, ln_b):
    weights = dict(w_in=w_in, conv_a_w=conv_a_w, conv_a_b=conv_a_b, conv_b_w=conv_b_w, conv_b_b=conv_b_b,
                   ln_b_g=ln_b_g, ln_b_b=ln_b_b, pool_w=pool_w, pool_b=pool_b, pool_scale=pool_scale,
                   w_out=w_out, ln_g=ln_g, ln_b=ln_b)
    weights = {k: np.asarray(v) for k, v in weights.items()}
    x = np.asarray(x, np.float32)
    if FUSED:
        return _run((0, 1), x, weights)
    for l in range(DEPTH):
        x = _run((l,), x, weights)
    return x
```
